# Optimizing a Trainium2 kernel written in Bass

```python
import jax, jax.numpy as jnp
from jax import lax
import numpy as np

D_MODEL = 2048
BATCH = 4
SEQ = 2048
DEPTH = 1
DEC_BATCH = 128
DEC_SEQ = 1
PAST_LEN = 16384
PAGE_SIZE = 128

D_MIX = D_MODEL
D_A = D_MIX // 2
D_B = D_MIX - D_A
EXPAND = 128
H_A = D_A // EXPAND
DK = EXPAND
DV = D_A // H_A
CONV_W = 31
D_FF = ((8 * D_MODEL // 3 + 255) // 256) * 256
D_IN = 4 * D_A + 2 * D_B
N_MOD = 9
CHUNK = 32
HALF = 0.5
EPS = 1e-6

kernel_name = "hymba_hgrn2_conformer_macaron_adaln_step"

F32 = jnp.float32


def _rmsnorm(x, g):
    xf = x.astype(F32)
    y = xf * lax.rsqrt(jnp.mean(xf * xf, axis=-1, keepdims=True) + EPS)
    return (y * g.astype(F32)).astype(x.dtype)


def _layernorm(x, g, b):
    xf = x.astype(F32)
    mu = jnp.mean(xf, axis=-1, keepdims=True)
    xc = xf - mu
    var = jnp.mean(xc * xc, axis=-1, keepdims=True)
    return (xc * lax.rsqrt(var + EPS) * g.astype(F32) + b.astype(F32)).astype(x.dtype)


def _modulate(h, shift, scale):
    return h * (1 + scale[:, None, :]) + shift[:, None, :]


def _swiglu(h, w_up, w_down):
    a, b = jnp.split(h @ w_up, 2, axis=-1)
    return (jax.nn.silu(a) * b) @ w_down


def _hgrn2(q, k, v, logf, s0):
    bsz, L = q.shape[0], q.shape[1]
    c = CHUNK if L >= CHUNK else L
    n = -(-L // c)
    pad = n * c - L
    if pad:
        pw = ((0, 0), (0, pad), (0, 0), (0, 0))
        q, k, v, logf = [jnp.pad(t, pw) for t in (q, k, v, logf)]

    def blocks(t):
        return t.reshape(bsz, n, c, H_A, t.shape[-1]).transpose(1, 0, 3, 2, 4)

    causal = jnp.tril(jnp.ones((c, c), dtype=bool))[:, :, None]

    def step(S, blk):
        qc, kc, vc, lf = blk
        b = jnp.cumsum(lf, axis=2)
        o_inter = jnp.einsum('bhtk,bhkv->bhtv', qc * jnp.exp(b), S)
        diff = b[:, :, :, None, :] - b[:, :, None, :, :]
        decay = jnp.exp(jnp.where(causal, diff, -jnp.inf))
        att = jnp.einsum('bhtsk,bhsk->bhts', qc[:, :, :, None, :] * decay, kc)
        o = o_inter + jnp.einsum('bhts,bhsv->bhtv', att, vc)
        b_last = b[:, :, -1:, :]
        S_new = jnp.exp(b_last[:, :, 0, :])[..., None] * S + jnp.einsum(
            'bhsk,bhsv->bhkv', kc * jnp.exp(b_last - b), vc)
        return S_new, o

    S, o = lax.scan(step, s0, (blocks(q), blocks(k), blocks(v), blocks(logf)))
    o = o.transpose(1, 0, 3, 2, 4).reshape(bsz, n * c, H_A, DV)[:, :L]
    return o, S


def _dwconv(u_ext, w, b):
    y = lax.conv_general_dilated(
        u_ext, w[:, None, :].astype(u_ext.dtype), window_strides=(1,), padding='VALID',
        dimension_numbers=('NWC', 'WIO', 'NWC'), feature_group_count=u_ext.shape[-1])
    return y + b


def _layer(x, c, s_hgrn, conv_buf, lb, w_ada, b_ada, norm_g, w_f1_up, w_f1_down, w_in, b_in,
           g_norm_a, conv_w, conv_b, ln_g, ln_b, w_out, w_f2_up, w_f2_down):
    bsz, L, _ = x.shape
    mod = jax.nn.silu(c) @ w_ada + b_ada
    sh1, sc1, gt1, sh2, sc2, gt2, sh3, sc3, gt3 = jnp.split(mod, N_MOD, axis=-1)

    h = _modulate(_rmsnorm(x, norm_g[0]), sh1, sc1)
    x = x + HALF * gt1[:, None, :] * _swiglu(h, w_f1_up, w_f1_down)

    h = _modulate(_rmsnorm(x, norm_g[1]), sh2, sc2)
    z = h @ w_in + b_in
    zq, zf, zi, zg, za, zb = jnp.split(
        z, [D_A, 2 * D_A, 3 * D_A, 4 * D_A, 4 * D_A + D_B], axis=-1)

    def heads(t):
        return t.astype(F32).reshape(bsz, L, H_A, -1)
    lbh = lb.reshape(H_A, DK)
    zf_h = heads(zf)
    q = jax.nn.silu(heads(zq))
    logf = jnp.logaddexp(jnp.log(lbh), jnp.log1p(-lbh) + jax.nn.log_sigmoid(zf_h))
    k = (1 - lbh) * jax.nn.sigmoid(-zf_h)
    v = heads(zi)
    o, s_new = _hgrn2(q, k, v, logf, s_hgrn.astype(F32))
    o = o * lax.rsqrt(jnp.mean(o * o, axis=-1, keepdims=True) + EPS)
    o = (o.reshape(bsz, L, D_A) * g_norm_a.astype(F32)
         * jax.nn.silu(zg.astype(F32))).astype(x.dtype)

    u = za * jax.nn.sigmoid(zb)
    u_ext = jnp.concatenate([conv_buf.astype(u.dtype), u], axis=1)
    y = jax.nn.silu(_layernorm(_dwconv(u_ext, conv_w, conv_b), ln_g, ln_b))

    mix = jnp.concatenate([o, y.astype(x.dtype)], axis=-1) @ w_out
    x = x + gt2[:, None, :] * mix

    h = _modulate(_rmsnorm(x, norm_g[2]), sh3, sc3)
    x = x + HALF * gt3[:, None, :] * _swiglu(h, w_f2_up, w_f2_down)
    return x, s_new, u_ext[:, -(CONV_W - 1):]


def setup_inputs(seed: int = 0) -> dict:
    key = jax.random.key(seed)
    ks = iter(jax.random.split(key, 32))
    nrm = lambda shape, s: jax.random.normal(next(ks), shape, F32) * s
    D = D_MODEL
    return {
        "x_prompt": nrm((BATCH, SEQ, D), 1.0),
        "x_sample": nrm((DEC_BATCH, DEC_SEQ, D), 1.0),
        "state_hgrn": nrm((DEPTH, DEC_BATCH, H_A, DK, DV), 0.5),
        "state_conv": nrm((DEPTH, DEC_BATCH, CONV_W - 1, D_B), 0.5),
        "c_prompt": nrm((BATCH, D), 1.0),
        "c_sample": nrm((DEC_BATCH, D), 1.0),
        "lb_logits": nrm((DEPTH + 1, D_A), 0.5),
        "w_ada": nrm((DEPTH, D, N_MOD * D), 0.5 * D ** -0.5),
        "b_ada": nrm((DEPTH, N_MOD * D), 0.05),
        "norm_g": 1.0 + nrm((DEPTH, 3, D), 0.05),
        "w_f1_up": nrm((DEPTH, D, 2 * D_FF), D ** -0.5),
        "w_f1_down": nrm((DEPTH, D_FF, D), D_FF ** -0.5),
        "w_in": nrm((DEPTH, D, D_IN), D ** -0.5),
        "b_in": nrm((DEPTH, D_IN), 0.02),
        "g_norm_a": 1.0 + nrm((DEPTH, D_A), 0.05),
        "conv_w": nrm((DEPTH, CONV_W, D_B), CONV_W ** -0.5),
        "conv_b": nrm((DEPTH, D_B), 0.02),
        "ln_g": 1.0 + nrm((DEPTH, D_B), 0.05),
        "ln_b": nrm((DEPTH, D_B), 0.02),
        "w_out": nrm((DEPTH, D_MIX, D), D_MIX ** -0.5),
        "w_f2_up": nrm((DEPTH, D, 2 * D_FF), D ** -0.5),
        "w_f2_down": nrm((DEPTH, D_FF, D), D_FF ** -0.5),
        "final_g": 1.0 + nrm((D,), 0.05),
    }


def reference(x_prompt, x_sample, state_hgrn, state_conv, c_prompt, c_sample, lb_logits,
              w_ada, b_ada, norm_g, w_f1_up, w_f1_down, w_in, b_in, g_norm_a, conv_w, conv_b,
              ln_g, ln_b, w_out, w_f2_up, w_f2_down, final_g):
    lbs = jnp.cumsum(jax.nn.softmax(lb_logits.astype(F32), axis=0), axis=0)
    bp = x_prompt.shape[0]
    s0 = jnp.zeros((bp, H_A, DK, DV), F32)
    buf0 = jnp.zeros((bp, CONV_W - 1, D_B), x_prompt.dtype)
    xp, xs = x_prompt, x_sample
    hp, hs, cp, cs = [], [], [], []
    for l in range(DEPTH):
        wl = (lbs[l], w_ada[l], b_ada[l], norm_g[l], w_f1_up[l], w_f1_down[l], w_in[l], b_in[l],
              g_norm_a[l], conv_w[l], conv_b[l], ln_g[l], ln_b[l], w_out[l], w_f2_up[l],
              w_f2_down[l])
        xp, sp, bufp = _layer(xp, c_prompt, s0, buf0, *wl)
        xs, ss, bufs = _layer(xs, c_sample, state_hgrn[l], state_conv[l], *wl)
        hp.append(sp)
        hs.append(ss)
        cp.append(bufp)
        cs.append(bufs)
    y_prompt = _rmsnorm(xp, final_g)
    y_sample = _rmsnorm(xs, final_g)
    new_hgrn_prompt = jnp.stack(hp, axis=0).astype(state_hgrn.dtype)
    new_hgrn_sample = jnp.stack(hs, axis=0).astype(state_hgrn.dtype)
    new_conv_prompt = jnp.stack(cp, axis=0).astype(state_conv.dtype)
    new_conv_sample = jnp.stack(cs, axis=0).astype(state_conv.dtype)
    return (y_prompt, y_sample, new_hgrn_prompt, new_hgrn_sample, new_conv_prompt, new_conv_sample)
```

```python
import numpy as np
from contextlib import ExitStack
import concourse.bass as bass
import concourse.mybir as mybir
from concourse.bass_utils import run_bass_kernel_spmd

F32 = mybir.dt.float32
BF16 = mybir.dt.bfloat16
ALU = mybir.AluOpType
AF = mybir.ActivationFunctionType

D = 2048; KC = 16; NP = 1024; NS = 16; NT = NP + NS
DFF = 5632; HC = 44; NG = 4; GJ = 11
NH = 8; CW = 31; HIST = 30
EPS = 1e-6
N_CORES = 8
TBS = [(0, 512), (512, 512)]

V_BADA = 0
V_BIN = 144
V_NG = 192
V_FG = 240
V_GA = 256
V_CB = 264
V_LNG = 272
V_LNB = 280
V_LB = 288
V_CW = 304
V_M = 552
V_OH = 553
V_SEL = 557
V_ID = 561
V_MASK = 689
NV = 817

XW = NH * 128 + NH * HIST


class Buf:
    __slots__ = ("name", "w", "r")

    def __init__(s, name):
        s.name = name; s.w = None; s.r = {}


class Slot:
    def __init__(s, sem):
        s.sem = sem; s.val = 0


class Sched:
    def __init__(s, nc, stack):
        s.nc = nc
        s.eng = {"pe": nc.tensor, "act": nc.scalar, "dve": nc.vector, "pool": nc.gpsimd, "sp": nc.sync}
        s.ops = {k: [] for k in s.eng}
        s.sem = {k: stack.enter_context(nc.semaphore("s_" + k)) for k in s.eng}
        s.cnt = {k: 0 for k in s.eng}
        s.waited = {k: {} for k in s.eng}
        s.stack = stack
        s.nslots = 0
        s.slots = []

    def slot(s):
        s.nslots += 1
        sl = Slot(s.stack.enter_context(s.nc.semaphore("d%d" % s.nslots)))
        s.slots.append(sl)
        return sl

    def _wait(s, e, ev):
        if ev is None:
            return
        sem, val, owner = ev
        if owner == e and e == "pe":
            return
        key = id(sem)
        if s.waited[e].get(key, 0) >= val:
            return
        s.waited[e][key] = val
        h = s.eng[e]
        s.ops[e].append(lambda: h.wait_ge(sem, val))

    def _deps(s, e, reads, writes):
        for b in reads:
            s._wait(e, b.w)
        for b in writes:
            s._wait(e, b.w)
            for ev in list(b.r.values()):
                s._wait(e, ev)

    def _post(s, ev, reads, writes):
        for b in reads:
            b.r[id(ev[0])] = ev
        for b in writes:
            b.w = ev; b.r = {}

    def op(s, e, fns, reads=(), writes=()):
        if callable(fns):
            fns = [fns]
        s._deps(e, reads, writes)
        s.cnt[e] += 1
        sem = s.sem[e]; ev = (sem, s.cnt[e], e)
        for f in fns[:-1]:
            s.ops[e].append(f)
        last = fns[-1]
        s.ops[e].append(lambda: last().then_inc(sem, 1))
        s._post(ev, reads, writes)
        return ev

    def dma(s, e, slot, fn, reads=(), writes=(), inc=16):
        s._deps(e, reads, writes)
        slot.val += inc
        sem = slot.sem; ev = (sem, slot.val, "dma")
        if inc == 1:
            s.ops[e].append(lambda: fn().then_inc(sem))
        else:
            s.ops[e].append(lambda: fn().then_inc(sem, inc))
        s._post(ev, reads, writes)
        return ev

    def run(s):
        nc = s.nc
        with nc.Block() as block:
            @block.sync
            def _(e):
                for f in s.ops["sp"]:
                    f()

            @block.scalar
            def _(e):
                for f in s.ops["act"]:
                    f()

            @block.vector
            def _(e):
                for f in s.ops["dve"]:
                    f()

            @block.gpsimd
            def _(e):
                for f in s.ops["pool"]:
                    f()

            @block.tensor
            def _(e):
                for f in s.ops["pe"]:
                    f()


class DummySched:
    def __init__(s):
        s.slots = []

    def slot(s):
        return Slot(None)

    def op(s, *a, **k):
        return None

    def dma(s, *a, **k):
        return None

    def _wait(s, *a, **k):
        return None

    def run(s):
        return None


def build_nc(stage=99, ncores=N_CORES):
    nc = bass.Bass("TRN2", target_bir_lowering=False)
    din = lambda name, shape: nc.dram_tensor(name, shape, F32, kind="ExternalInput").ap()
    dout = lambda name, shape: nc.dram_tensor(name, shape, F32, kind="ExternalOutput").ap()
    xT_d = din("xT", [D, NT])
    xTp_d = din("xTp", [D, NT])
    cT_d = din("cT", [D, 17])
    vec_d = din("vec", [128, NV])
    wada_d = din("wada", [144, 128, 2048])
    wup1_d = din("wup1", [88, 128, 2048]); wdn1_d = din("wdn1", [64, 128, GJ * 128])
    wup2_d = din("wup2", [88, 128, 2048]); wdn2_d = din("wdn2", [64, 128, GJ * 128])
    win_d = din("win", [48, 128, 2048])
    wout_d = din("wout", [32, 128, 1024])
    shg_d = din("shg", [128, NS, NH, 128])
    scv_d = din("scv", [128, NH, NS, HIST])
    yT_o = dout("yT", [D, NT])
    nhp_o = dout("nhp", [128, NH, 128])
    nhs_o = dout("nhs", [128, NS, NH, 128])
    ncp_o = dout("ncp", [128, NH, HIST])
    ncs_o = dout("ncs", [128, NH, NS, HIST])
    scr = nc.dram_tensor("scr", [128, 1024], F32)
    dbg = dout("dbg", [128, KC, NT]) if stage < 99 else None

    with ExitStack() as st:
        S_real = Sched(nc, st)
        sb = lambda name, shape, dt=F32: st.enter_context(nc.sbuf_tensor(name, shape, dt))
        ps = lambda name, shape: st.enter_context(nc.psum_tensor(name, shape, F32))

        XT = sb("XT", [128, KC, NT]); bXT = [Buf("xt%d" % k) for k in range(KC)]
        HT = sb("HT", [128, KC, NT], BF16); bHT = Buf("ht"); bHTs = Buf("hts")
        VEC = sb("VEC", [128, NV]); bVEC = Buf("vec")
        MOD = sb("MOD", [128, 144, 17]); bMODm = [Buf("mod%d" % m) for m in range(9)]
        bCT = Buf("ct")
        SMT = sb("SMT", [128, 128]); bSMT = Buf("smt")
        HR = sb("HR", [128, KC, HIST], BF16); bHR = Buf("hr")
        STG = [sb("STG%d" % i, [128, 2048]) for i in range(2)]; bSTG = [Buf("stg%d" % i) for i in range(2)]
        WB = [sb("WB%d" % i, [128, 2048], BF16) for i in range(2)]; bWB = [Buf("wb%d" % i) for i in range(2)]
        ONESB = sb("ONESB", [128, 128], BF16); IDB = sb("IDB", [128, 128], BF16)
        EPSC = sb("EPSC", [128, 1]); bCONST = Buf("const")
        RSTD = sb("RSTD", [128, NT]); bRSTD = Buf("rstd")
        SQ = [sb("SQ%d" % i, [128, NT], BF16) for i in range(2)]; bSQ = [Buf("sq%d" % i) for i in range(2)]
        TMP = [sb("TMP%d" % i, [128, NT]) for i in range(2)]; bTMP = [Buf("tmp%d" % i) for i in range(2)]
        T16 = [sb("T16_%d" % i, [128, NS]) for i in range(2)]; bT16 = [Buf("t16_%d" % i) for i in range(2)]
        ARW = 13824
        AR = sb("AR", [128, ARW])
        CTf = AR[:, 13000:13272].rearrange("p (k n) -> p k n", n=17)
        CTbt = sb("CTbt", [128, KC, 17], BF16); CTb = CTbt[:]
        G = AR[:, 0:GJ * NT // 2].bitcast(BF16).rearrange("p (j t) -> p j t", j=GJ)
        bG = [Buf("g%d" % j) for j in range(GJ)]
        for b_ in bG:
            pass

        PB = [ps("PB0", [128, 1024]), ps("PB1", [128, 1024])]
        SM = [ps("SM0", [128, 512]), ps("SM1", [128, 512])]
        PST = ps("PST", [128, 1024])
        bPB = [Buf("pb0"), Buf("pb1")]; bPSTa = Buf("psta"); bPSTb = Buf("pstb"); bPSM = [Buf("sm0"), Buf("sm1")]
        PSMv = [SM[0][:, 0:16], SM[1][:, 0:16]]

        plan = []

        def program(S, planning):
            SP = "sp"
            ld = [S.slot() for _ in range(4)]
            stg_slot = [S.slot(), S.slot()]
            out_slots = []

            S.dma(SP, ld[0], lambda: nc.sync.dma_start(out=VEC[:], in_=vec_d), writes=[bVEC])
            S.dma(SP, ld[1], lambda: nc.sync.dma_start(out=CTf[:], in_=cT_d.rearrange("(k p) n -> p k n", p=128)), writes=[bCT])
            S.op("dve", [lambda: nc.vector.memset(ONESB[:], 1.0), lambda: nc.vector.memset(EPSC[:], EPS),
                         lambda: nc.vector.tensor_copy(out=IDB[:], in_=VEC[:, V_ID:V_ID + 128])], reads=[bVEC], writes=[bCONST])
            S.op("act", lambda: nc.scalar.activation(out=CTb[:], in_=CTf[:], func=AF.Silu), reads=[], writes=[bCT])
            for k in range(KC):
                pass

            arbufs = list(bG) + [bCT]

            def merge_ev(b, ev):
                k_ = id(ev[0])
                if k_ not in b.r or b.r[k_][1] < ev[1]:
                    b.r[k_] = ev

            def alias_into(b, olds):
                for o in olds:
                    if o is b:
                        continue
                    if o.w is not None:
                        merge_ev(b, o.w)
                    for ev in o.r.values():
                        merge_ev(b, ev)

            def newbuf(name):
                b = Buf(name)
                alias_into(b, arbufs)
                arbufs.append(b)
                return b

            arf = lambda off, n: AR[:, off:off + n]
            arb = lambda off, nb: AR[:, off:off + nb // 2].bitcast(BF16)

            ws = {"dma": 0, "cast": 0, "use": 0}
            WCLS = {"ffn": False}
            STGS = STG + [AR[:, 5720:7768], AR[:, 7768:9816]]
            bSTGS = bSTG + [newbuf("ars2"), newbuf("ars3")]
            WBS = WB + [AR[:, 9816:10840].bitcast(BF16), AR[:, 10840:11864].bitcast(BF16)]
            bWBS = bWB + [newbuf("arw2"), newbuf("arw3")]
            stg_slots4 = stg_slot + [S.slot(), S.slot()]
            slot_of = {}; wb_of = {}
            last_user = {}; wb_last = {}

            def is_deep(k):
                return all(0 <= k - d < len(plan) and plan[k - d][2] for d in range(0, 9))

            def ws_dma(i):
                ap, w, _ = plan[i]
                n_ = 4 if is_deep(i) else 2
                s_ = min(range(n_), key=lambda c: last_user.get(c, -1))
                slot_of[i] = s_; last_user[s_] = i
                S.dma(SP, stg_slots4[s_], lambda: nc.sync.dma_start(out=STGS[s_][:, 0:w], in_=ap), writes=[bSTGS[s_]])

            def dma_ok(k):
                n_ = 4 if is_deep(k) else 2
                return min(last_user.get(c, -1) for c in range(n_)) < ws["cast"]

            def ws_cast(i):
                ap, w, _ = plan[i]
                s_ = slot_of[i]
                n_ = 4 if is_deep(i) else 2
                wq = min(range(n_), key=lambda c: wb_last.get(c, -1))
                wb_of[i] = wq; wb_last[wq] = i
                if i % 2 == 0:
                    S.op("act", lambda: nc.scalar.copy(out=WBS[wq][:, 0:w], in_=STGS[s_][:, 0:w]), reads=[bSTGS[s_]], writes=[bWBS[wq]])
                else:
                    S.op("dve", lambda: nc.vector.tensor_copy(out=WBS[wq][:, 0:w], in_=STGS[s_][:, 0:w]), reads=[bSTGS[s_]], writes=[bWBS[wq]])

            def cast_ok(k, i):
                n_ = 4 if is_deep(k) else 2
                return min(wb_last.get(c, -1) for c in range(n_)) < i

            def ws_next(ap, w):
                i = ws["use"]
                ws["use"] += 1
                if planning:
                    plan.append((ap, w, WCLS["ffn"]))
                    return WB[i % 2][:, 0:w].rearrange("p (k n) -> p k n", n=128), bWB[i % 2]
                assert plan[i][1] == w
                while ws["cast"] < min(len(plan), i + 4) and (ws["cast"] <= i or cast_ok(ws["cast"], i)):
                    k = ws["cast"]
                    assert cast_ok(k, i)
                    while ws["dma"] <= k:
                        ws_dma(ws["dma"]); ws["dma"] += 1
                    ws_cast(k); ws["cast"] += 1
                while ws["dma"] < min(len(plan), i + 8) and dma_ok(ws["dma"]):
                    ws_dma(ws["dma"]); ws["dma"] += 1
                wq = wb_of[i]
                return WBS[wq][:, 0:w].rearrange("p (k n) -> p k n", n=128), bWBS[wq]

            job_ctr = {"n": 0}
            FL = {"small": True}

            def proj(tile, rhs_fn, nk, rbufs, small=True, big=True, hist=False):
                small = small and FL["small"]
                wv, wbuf = ws_next(tile, nk * 128)
                j = job_ctr["n"] % 2; job_ctr["n"] += 1
                fns = []
                for k in range(nk):
                    if big:
                        for bi, (t0, tn) in enumerate(TBS):
                            fns.append((lambda k=k, bi=bi, t0=t0, tn=tn: nc.tensor.matmul(
                                PB[j][:, bi * 512:(bi + 1) * 512], lhsT=wv[:, k, :], rhs=rhs_fn(k, t0, tn),
                                start=(k == 0), stop=(k == nk - 1))))
                    if small:
                        fns.append((lambda k=k: nc.tensor.matmul(
                            PSMv[j], lhsT=wv[:, k, :], rhs=rhs_fn(k, NP, NS), start=(k == 0), stop=(k == nk - 1))))
                if hist:
                    for k in range(nk):
                        fns.append((lambda k=k: nc.tensor.matmul(
                            SM[j][:, 32:32 + HIST], lhsT=wv[:, k, :], rhs=HR[:, k, :], start=(k == 0), stop=(k == nk - 1))))
                wr = ([bPB[j]] if big else []) + ([bPSM[j]] if (small or hist) else [])
                S.op("pe", fns, reads=[wbuf] + list(rbufs) + ([bHR] if hist else []), writes=wr)
                return j

            ht_rhs = lambda k, t0, tn: HT[:, k, t0:t0 + tn]

            mod_state = {"next": 0}

            def mod_job(limit=144):
                oc = mod_state["next"]
                if oc >= limit:
                    return False
                mod_state["next"] = oc + 1
                m = oc // 16
                wv, wbuf = ws_next(wada_d[oc], 2048)
                fns = [(lambda k=k: nc.tensor.matmul(SM[1][:, 256:273], lhsT=wv[:, k, :], rhs=CTb[:, k, :],
                                                     start=(k == 0), stop=(k == KC - 1))) for k in range(KC)]
                S.op("pe", fns, reads=[wbuf, bCT], writes=[bPSM[1]])
                S.op("act", lambda: nc.scalar.activation(out=MOD[:, oc, :], in_=SM[1][:, 256:273], func=AF.Identity,
                                                         bias=VEC[:, V_BADA + oc:V_BADA + oc + 1], scale=1.0),
                     reads=[bPSM[1], bVEC], writes=[bMODm[m]])
                if oc % 16 == 15:
                    blk = MOD[:, m * 16:(m + 1) * 16, :]
                    if m in (1, 4, 7):
                        i = m // 3
                        ngb = VEC[:, V_NG + i * 16:V_NG + (i + 1) * 16].unsqueeze(2).to_broadcast([128, 16, 17])
                        S.op("dve", lambda: nc.vector.scalar_tensor_tensor(out=blk, in0=blk, scalar=1.0, in1=ngb, op0=ALU.add, op1=ALU.mult),
                             reads=[bVEC], writes=[bMODm[m]])
                    if m in (2, 8):
                        S.op("dve", lambda: nc.vector.tensor_scalar(out=blk, in0=blk, scalar1=0.5, scalar2=None, op0=ALU.mult),
                             reads=[], writes=[bMODm[m]])
                return True

            WCLS["ffn"] = True
            for _ in range(32):
                mod_job()
            WCLS["ffn"] = False
            PC = 16
            mod_p = lambda m, k: MOD[:, m * 16 + k, PC:PC + 1]
            mod_s = lambda m, k: MOD[:, m * 16 + k, 0:NS]

            def rms_stats():
                for k in range(KC):
                    q = k % 2
                    W_ = NT if FL["small"] else NP
                    S.op("act", lambda k=k, q=q, W_=W_: nc.scalar.activation(out=SQ[q][:, 0:W_], in_=XT[:, k, 0:W_], func=AF.Square),
                         reads=[bXT[k]], writes=[bSQ[q]])
                    fns = [(lambda q=q, bi=bi, t0=t0, tn=tn, k=k: nc.tensor.matmul(
                        PST[:, bi * 512:(bi + 1) * 512], lhsT=ONESB[:], rhs=SQ[q][:, t0:t0 + tn], start=(k == 0), stop=(k == KC - 1)))
                        for bi, (t0, tn) in enumerate(TBS)]
                    if FL["small"]:
                        fns.append(lambda q=q, k=k: nc.tensor.matmul(SM[0][:, 0:NS], lhsT=ONESB[:], rhs=SQ[q][:, NP:NT],
                                                                     start=(k == 0), stop=(k == KC - 1)))
                    S.op("pe", fns, reads=[bSQ[q], bCONST], writes=[bPSTa, bPSTb, bPSM[0]])
                fa = [lambda: nc.scalar.activation(out=RSTD[:, 0:NP], in_=PST[:], func=AF.Sqrt, bias=EPSC[:], scale=1.0 / D)]
                if FL["small"]:
                    fa.append(lambda: nc.scalar.activation(out=RSTD[:, NP:NT], in_=SM[0][:, 0:NS], func=AF.Sqrt, bias=EPSC[:], scale=1.0 / D))
                S.op("act", fa, reads=[bPSTa, bPSTb, bPSM[0], bCONST], writes=[bRSTD])
                W_ = NT if FL["small"] else NP
                S.op("dve", lambda: nc.vector.reciprocal(out=RSTD[:, 0:W_], in_=RSTD[:, 0:W_]), reads=[], writes=[bRSTD])

            def norm_mod(i):
                rms_stats()
                for k in range(KC):
                    q = k % 2
                    W_ = NT if FL["small"] else NP
                    S.op("dve", lambda k=k, q=q, W_=W_: nc.vector.tensor_tensor(out=TMP[q][:, 0:W_], in0=XT[:, k, 0:W_], in1=RSTD[:, 0:W_], op=ALU.mult),
                         reads=[bXT[k], bRSTD], writes=[bTMP[q]])
                    S.op("act", lambda k=k, q=q: nc.scalar.activation(out=HT[:, k, 0:NP], in_=TMP[q][:, 0:NP], func=AF.Identity,
                                                                      bias=mod_p(3 * i, k), scale=mod_p(3 * i + 1, k)),
                         reads=[bTMP[q], bMODm[3 * i], bMODm[3 * i + 1]], writes=[bHT])
                    if not FL["small"]:
                        continue
                    S.op("dve", lambda k=k, q=q: nc.vector.tensor_tensor(out=T16[q][:], in0=TMP[q][:, NP:NT], in1=mod_s(3 * i + 1, k), op=ALU.mult),
                         reads=[bTMP[q], bMODm[3 * i + 1]], writes=[bT16[q]])
                    S.op("dve", lambda k=k, q=q: nc.vector.tensor_tensor(out=HT[:, k, NP:NT], in0=T16[q][:], in1=mod_s(3 * i, k), op=ALU.add),
                         reads=[bT16[q], bMODm[3 * i]], writes=[bHTs])

            def resid_evac(j, n, m):
                if not FL["small"]:
                    S.op("dve", lambda: nc.vector.scalar_tensor_tensor(out=XT[:, n, 0:NP], in0=PB[j][:], scalar=mod_p(m, n),
                                                                       in1=XT[:, n, 0:NP], op0=ALU.mult, op1=ALU.add),
                         reads=[bPB[j], bMODm[m]], writes=[bXT[n]])
                    return
                S.op("dve", [lambda: nc.vector.scalar_tensor_tensor(out=XT[:, n, 0:NP], in0=PB[j][:], scalar=mod_p(m, n),
                                                                    in1=XT[:, n, 0:NP], op0=ALU.mult, op1=ALU.add),
                             lambda: nc.vector.tensor_tensor(out=T16[j][:], in0=PSMv[j], in1=mod_s(m, n), op=ALU.mult)],
                     reads=[bPB[j], bPSM[j], bMODm[m]], writes=[bXT[n], bT16[j]])
                S.op("dve", lambda: nc.vector.tensor_tensor(out=XT[:, n, NP:NT], in0=XT[:, n, NP:NT], in1=T16[j][:], op=ALU.add),
                     reads=[bT16[j]], writes=[bXT[n]])

            def ffn(i, wu, wd, bg=None):
                m_gate = 3 * i + 2
                WCLS["ffn"] = True
                for b_ in list(bG) + bSTGS[2:] + bWBS[2:]:
                    alias_into(b_, arbufs)
                for g in range(NG):
                    for jj in range(GJ):
                        hc = g * GJ + jj
                        ja = proj(wu[2 * hc], ht_rhs, KC, [bHT, bHTs])
                        fa = [lambda ja=ja: nc.scalar.activation(out=TMP[ja][:, 0:NP], in_=PB[ja][:], func=AF.Silu)]
                        if FL["small"]:
                            fa.append(lambda ja=ja: nc.scalar.activation(out=TMP[ja][:, NP:NT], in_=PSMv[ja], func=AF.Silu))
                        S.op("act", fa, reads=[bPB[ja], bPSM[ja]], writes=[bTMP[ja]])
                        if bg is not None:
                            bg()
                        jb = proj(wu[2 * hc + 1], ht_rhs, KC, [bHT, bHTs])
                        fd = [lambda ja=ja, jb=jb, jj=jj: nc.vector.tensor_tensor(out=G[:, jj, 0:NP], in0=PB[jb][:], in1=TMP[ja][:, 0:NP], op=ALU.mult)]
                        if FL["small"]:
                            fd.append(lambda ja=ja, jb=jb, jj=jj: nc.vector.tensor_tensor(out=G[:, jj, NP:NT], in0=PSMv[jb], in1=TMP[ja][:, NP:NT], op=ALU.mult))
                        S.op("dve", fd, reads=[bPB[jb], bPSM[jb], bTMP[ja]], writes=[bG[jj]])
                        if bg is not None:
                            bg()
                    g_rhs = lambda k, t0, tn: G[:, k, t0:t0 + tn]
                    for n in range(KC):
                        j = proj(wd[g * KC + n], g_rhs, GJ, bG)
                        resid_evac(j, n, m_gate)
                WCLS["ffn"] = False

            def finish(src_ap=None, rbufs=()):
                osl = S.slot()
                if src_ap is not None:
                    S.dma(SP, osl, lambda: nc.sync.dma_start(out=dbg, in_=src_ap), reads=list(rbufs))
                if planning:
                    return
                for sl in S.slots:
                    if sl.val:
                        S._wait(SP, (sl.sem, sl.val, "dma"))
                S.run()

            S.dma(SP, ld[3], lambda: nc.sync.dma_start(out=XT[:], in_=xTp_d.rearrange("(k p) t -> p k t", p=128)), writes=bXT)
            FL["small"] = False
            norm_mod(0)
            if stage == 0:
                S.op("dve", lambda: nc.vector.tensor_copy(out=XT[:], in_=HT[:]), reads=[bHT, bHTs], writes=bXT)
                finish(XT[:], bXT); return
            ffn(0, wup1_d, wdn1_d, bg=lambda: mod_job(80))
            WCLS["ffn"] = True
            while mod_job(80):
                pass
            WCLS["ffn"] = False
            norm_mod(1)
            S.op("dve", lambda: nc.vector.tensor_copy(out=HR[:], in_=HT[:, :, NP - HIST:NP]), reads=[bHT], writes=[bHR])
            IDF = VEC[:, V_ID:V_ID + 128]
            MASKF = VEC[:, V_MASK:V_MASK + 128]
            bin_col = lambda oc: VEC[:, V_BIN + oc:V_BIN + oc + 1]
            LBv = SMT[:, 0:8]; OML = SMT[:, 8:16]; BEND = SMT[:, 16:17]; FS = SMT[:, 17:33]; EBLt = SMT[:, 33:65]
            S.op("dve", lambda: nc.vector.tensor_tensor(out=LBv, in0=VEC[:, V_LB:V_LB + 8], in1=VEC[:, V_LB + 8:V_LB + 16], op=ALU.subtract),
                 reads=[bVEC], writes=[bSMT])
            S.op("act", lambda: nc.scalar.activation(out=LBv, in_=LBv, func=AF.Sigmoid), reads=[], writes=[bSMT])
            S.op("dve", lambda: nc.vector.tensor_scalar(out=OML, in0=LBv, scalar1=-1.0, scalar2=1.0, op0=ALU.mult, op1=ALU.add),
                 reads=[], writes=[bSMT])

            Fb = arf(4224, 1024); Vb = arf(5248, 1024)
            KHT = arf(6272, 1024).rearrange("p (t d) -> p t d", t=8); VT = arf(7296, 1024).rearrange("p (t d) -> p t d", t=8)
            SU = arf(8320, XW); SEND = SU[:, 0:1024].rearrange("p (h d) -> p h d", h=8)
            bFb = newbuf("fb"); bVb = newbuf("vb"); bKHT = newbuf("kht"); bVT = newbuf("vt"); bSU = newbuf("su")
            ONES1K = SQ[1][:, 0:NP]
            S.op("dve", lambda: nc.vector.memset(ONES1K, 1.0), reads=[], writes=[bSQ[1]])

            def transposes_to(src, bsrc, dst, bdst):
                fns = [(lambda t=t: nc.tensor.transpose(PST[:, t * 128:(t + 1) * 128], src[:, t * 128:(t + 1) * 128], IDF)) for t in range(8)]
                S.op("pe", fns, reads=[bsrc, bVEC], writes=[bPSTa, bPSTb])
                S.op("act", lambda: nc.scalar.copy(out=dst.rearrange("p t d -> p (t d)"), in_=PST[:]), reads=[bPSTa, bPSTb], writes=[bdst])

            for h in range(NH):
                jf = proj(win_d[8 + h], ht_rhs, KC, [bHT], small=False)
                S.op("act", lambda jf=jf, h=h: nc.scalar.activation(out=Fb, in_=PB[jf][:], func=AF.Sigmoid, bias=bin_col(8 + h), scale=1.0),
                     reads=[bPB[jf], bVEC], writes=[bFb])
                S.op("dve", lambda h=h: nc.vector.tensor_scalar(out=Fb, in0=Fb, scalar1=OML[:, h:h + 1], scalar2=LBv[:, h:h + 1], op0=ALU.mult, op1=ALU.add),
                     reads=[bSMT], writes=[bFb])
                ji = proj(win_d[16 + h], ht_rhs, KC, [bHT], small=False)
                S.op("act", lambda ji=ji, h=h: nc.scalar.activation(out=Vb, in_=PB[ji][:], func=AF.Identity, bias=bin_col(16 + h), scale=1.0),
                     reads=[bPB[ji], bVEC], writes=[bVb])
                S.op("act", lambda: nc.scalar.activation(out=TMP[0][:, 0:NP], in_=Fb, func=AF.Ln), reads=[bFb], writes=[bTMP[0]])
                S.op("dve", lambda: nc.vector.tensor_tensor_scan(out=TMP[1][:, 0:NP], data0=ONES1K, data1=TMP[0][:, 0:NP], initial=0.0, op0=ALU.mult, op1=ALU.add),
                     reads=[bTMP[0], bSQ[1]], writes=[bTMP[1]])
                S.op("dve", lambda: nc.vector.tensor_copy(out=BEND, in_=TMP[1][:, NP - 1:NP]), reads=[bTMP[1]], writes=[bSMT])
                S.op("dve", lambda: nc.vector.tensor_scalar(out=TMP[1][:, 0:NP], in0=TMP[1][:, 0:NP], scalar1=BEND, scalar2=-1.0, op0=ALU.subtract, op1=ALU.mult),
                     reads=[bSMT], writes=[bTMP[1]])
                S.op("act", lambda: nc.scalar.activation(out=RSTD[:, 0:NP], in_=TMP[1][:, 0:NP], func=AF.Exp), reads=[bTMP[1]], writes=[bRSTD])
                S.op("dve", lambda: nc.vector.tensor_scalar(out=Fb, in0=Fb, scalar1=-1.0, scalar2=1.0, op0=ALU.mult, op1=ALU.add), reads=[], writes=[bFb])
                S.op("dve", lambda: nc.vector.tensor_tensor(out=Fb, in0=Fb, in1=RSTD[:, 0:NP], op=ALU.mult), reads=[bRSTD], writes=[bFb])
                transposes_to(Fb, bFb, KHT, bKHT)
                transposes_to(Vb, bVb, VT, bVT)
                fns = [(lambda t=t: nc.tensor.matmul(SM[0][:, 0:128], lhsT=KHT[:, t, :], rhs=VT[:, t, :], start=(t == 0), stop=(t == 7))) for t in range(8)]
                S.op("pe", fns, reads=[bKHT, bVT], writes=[bPSM[0]])
                S.op("act", lambda h=h: nc.scalar.copy(out=SEND[:, h, :], in_=SM[0][:, 0:128]), reads=[bPSM[0]], writes=[bSU])
            S.op("dve", lambda: nc.vector.tensor_scalar(out=SU[:, 0:1024], in0=SU[:, 0:1024], scalar1=VEC[:, V_M:V_M + 1], scalar2=None, op0=ALU.mult),
                 reads=[bVEC], writes=[bSU])
            scr_slot = S.slot(); bSCR = Buf("scr")
            S.dma(SP, scr_slot, lambda: nc.sync.dma_start(out=scr.ap(), in_=SU[:, 0:1024]), reads=[bSU], writes=[bSCR])

            FL["small"] = True
            S.dma(SP, ld[2], lambda: nc.sync.dma_start(out=XT[:], in_=xT_d.rearrange("(k p) t -> p k t", p=128)), writes=bXT)
            norm_mod(0)
            for b_ in bG:
                alias_into(b_, arbufs)
            ffn(0, wup1_d, wdn1_d)
            if stage == 1:
                finish(XT[:], bXT); return
            norm_mod(1)

            UTb = arb(0, 8 * 1056).rearrange("p (c t) -> p c t", c=8)
            Y = arf(4224, 8192).rearrange("p (c t) -> p c t", c=8)
            O_SP = 12416
            UTAIL = arf(O_SP, 240).rearrange("p (c t) -> p c t", c=8)
            USs = arf(O_SP + 240, 128).rearrange("p (c t) -> p c t", c=8)
            YS = arf(O_SP + 368, 128).rearrange("p (c t) -> p c t", c=8)
            DIAG = [arb(O_SP + 496 + 64 * i, 128) for i in range(2)]
            SCc = arf(O_SP + 624, 496).rearrange("p (n j) -> p n j", j=31)
            bUTB = newbuf("utb"); bY = [newbuf("y%d" % c) for c in range(8)]; bSML = newbuf("sml"); bYS = newbuf("ys")
            bDIAG = [newbuf("diag0"), newbuf("diag1")]; bSCc = newbuf("scc")
            sc_slot = S.slot(); ncs_slot = S.slot()
            PROD = RSTD[:, 0:496].rearrange("p (n j) -> p n j", j=31)
            HA = SMT[:, 65:65 + HIST]; HB = SMT[:, 96:96 + HIST]; bHAB = Buf("hab")
            for c in range(8):
                ja = proj(win_d[32 + c], ht_rhs, KC, [bHT, bHTs], hist=True)
                S.op("act", [lambda ja=ja, c=c: nc.scalar.activation(out=TMP[0][:, 0:NP], in_=PB[ja][:], func=AF.Identity, bias=bin_col(32 + c), scale=1.0),
                             lambda ja=ja, c=c: nc.scalar.activation(out=TMP[0][:, NP:NT], in_=PSMv[ja], func=AF.Identity, bias=bin_col(32 + c), scale=1.0),
                             lambda ja=ja, c=c: nc.scalar.activation(out=HA, in_=SM[ja][:, 32:32 + HIST], func=AF.Identity, bias=bin_col(32 + c), scale=1.0)],
                     reads=[bPB[ja], bPSM[ja], bVEC], writes=[bTMP[0], bHAB])
                mod_job(96)
                jb = proj(win_d[40 + c], ht_rhs, KC, [bHT, bHTs], hist=True)
                mod_job(96)
                S.op("act", [lambda jb=jb, c=c: nc.scalar.activation(out=TMP[1][:, 0:NP], in_=PB[jb][:], func=AF.Sigmoid, bias=bin_col(40 + c), scale=1.0),
                             lambda jb=jb, c=c: nc.scalar.activation(out=TMP[1][:, NP:NT], in_=PSMv[jb], func=AF.Sigmoid, bias=bin_col(40 + c), scale=1.0),
                             lambda jb=jb, c=c: nc.scalar.activation(out=HB, in_=SM[jb][:, 32:32 + HIST], func=AF.Sigmoid, bias=bin_col(40 + c), scale=1.0)],
                     reads=[bPB[jb], bPSM[jb], bVEC], writes=[bTMP[1], bHAB])
                S.op("dve", [lambda c=c: nc.vector.tensor_tensor(out=UTb[:, c, HIST:HIST + NP], in0=TMP[0][:, 0:NP], in1=TMP[1][:, 0:NP], op=ALU.mult),
                             lambda c=c: nc.vector.tensor_tensor(out=USs[:, c, :], in0=TMP[0][:, NP:NT], in1=TMP[1][:, NP:NT], op=ALU.mult),
                             lambda c=c: nc.vector.tensor_tensor(out=UTAIL[:, c, :], in0=TMP[0][:, NP - HIST:NP], in1=TMP[1][:, NP - HIST:NP], op=ALU.mult),
                             lambda c=c: nc.vector.scalar_tensor_tensor(out=UTb[:, c, 0:HIST], in0=HA, scalar=VEC[:, V_M:V_M + 1], in1=HB, op0=ALU.mult, op1=ALU.mult)],
                     reads=[bTMP[0], bTMP[1], bHAB, bVEC], writes=[bUTB, bSML])
                S.dma(SP, sc_slot, lambda c=c: nc.sync.dma_start(out=SCc[:, :, 0:HIST], in_=scv_d[:, c, :, :]), writes=[bSCc])
                S.op("dve", lambda c=c: nc.vector.tensor_copy(out=SCc[:, :, HIST:HIST + 1], in_=USs[:, c, :].unsqueeze(2)), reads=[bSML], writes=[bSCc])
                cwb = VEC[:, V_CW + c * CW:V_CW + (c + 1) * CW].unsqueeze(1).to_broadcast([128, NS, CW])
                S.op("dve", lambda c=c, cwb=cwb: nc.vector.tensor_tensor(out=PROD, in0=SCc, in1=cwb, op=ALU.mult), reads=[bSCc, bVEC], writes=[bRSTD])
                S.op("dve", lambda c=c: nc.vector.tensor_reduce(out=YS[:, c, :], in_=PROD, axis=mybir.AxisListType.X, op=ALU.add), reads=[bRSTD], writes=[bYS])
                S.op("dve", lambda c=c: nc.vector.tensor_scalar(out=YS[:, c, :], in0=YS[:, c, :], scalar1=VEC[:, V_CB + c:V_CB + c + 1], scalar2=None, op0=ALU.add),
                     reads=[bVEC], writes=[bYS])
                S.dma(SP, ncs_slot, lambda c=c: nc.sync.dma_start(out=ncs_o[:, c, :, :], in_=SCc[:, :, 1:CW]), reads=[bSCc])
            S.dma(SP, ncs_slot, lambda: nc.sync.dma_start(out=ncp_o, in_=UTAIL), reads=[bSML])

            for c in range(8):
                sl_ = c % 2
                for j in range(CW):
                    dq = j % 2
                    S.op("dve", lambda c=c, j=j, dq=dq: nc.vector.tensor_scalar(out=DIAG[dq], in0=IDB[:], scalar1=VEC[:, V_CW + c * CW + j:V_CW + c * CW + j + 1],
                                                                               scalar2=None, op0=ALU.mult), reads=[bVEC, bCONST], writes=[bDIAG[dq]])
                    fns = [(lambda c=c, j=j, dq=dq, tb=tb, sl_=sl_: nc.tensor.matmul(PB[sl_][:, tb * 512:(tb + 1) * 512], lhsT=DIAG[dq],
                                                                                 rhs=UTb[:, c, j + tb * 512:j + tb * 512 + 512],
                                                                                 start=(j == 0), stop=(j == CW - 1))) for tb in range(2)]
                    S.op("pe", fns, reads=[bDIAG[dq], bUTB], writes=[bPB[sl_]])
                S.op("act", lambda c=c, sl_=sl_: nc.scalar.activation(out=Y[:, c, :], in_=PB[sl_][:], func=AF.Identity, bias=VEC[:, V_CB + c:V_CB + c + 1], scale=1.0),
                     reads=[bPB[sl_], bVEC], writes=[bY[c]])

            for c in range(8):
                S.op("act", [lambda c=c: nc.scalar.copy(out=SQ[0][:, 0:NP], in_=Y[:, c, :]),
                             lambda c=c: nc.scalar.copy(out=SQ[0][:, NP:NT], in_=YS[:, c, :])], reads=[bY[c], bYS], writes=[bSQ[0]])
                S.op("act", [lambda c=c: nc.scalar.activation(out=SQ[1][:, 0:NP], in_=Y[:, c, :], func=AF.Square),
                             lambda c=c: nc.scalar.activation(out=SQ[1][:, NP:NT], in_=YS[:, c, :], func=AF.Square)], reads=[bY[c], bYS], writes=[bSQ[1]])
                fns = []
                for tb in range(2):
                    fns.append(lambda c=c, tb=tb: nc.tensor.matmul(PST[:, tb * 512:(tb + 1) * 512], lhsT=ONESB[:], rhs=SQ[0][:, tb * 512:(tb + 1) * 512], start=(c == 0), stop=(c == 7)))
                    fns.append(lambda c=c, tb=tb: nc.tensor.matmul(PB[0][:, tb * 512:(tb + 1) * 512], lhsT=ONESB[:], rhs=SQ[1][:, tb * 512:(tb + 1) * 512], start=(c == 0), stop=(c == 7)))
                fns.append(lambda c=c: nc.tensor.matmul(SM[0][:, 0:NS], lhsT=ONESB[:], rhs=SQ[0][:, NP:NT], start=(c == 0), stop=(c == 7)))
                fns.append(lambda c=c: nc.tensor.matmul(SM[1][:, 0:NS], lhsT=ONESB[:], rhs=SQ[1][:, NP:NT], start=(c == 0), stop=(c == 7)))
                S.op("pe", fns, reads=[bSQ[0], bSQ[1], bCONST], writes=[bPSTa, bPSTb, bPB[0], bPSM[0], bPSM[1]])
            MEAN = TMP[0]; MSQ = TMP[1]
            S.op("act", [lambda: nc.scalar.mul(out=MEAN[:, 0:NP], in_=PST[:], mul=1.0 / 1024), lambda: nc.scalar.mul(out=MEAN[:, NP:NT], in_=SM[0][:, 0:NS], mul=1.0 / 1024)],
                 reads=[bPSTa, bPSTb, bPSM[0]], writes=[bTMP[0]])
            S.op("act", [lambda: nc.scalar.mul(out=RSTD[:, 0:NP], in_=PB[0][:], mul=1.0 / 1024), lambda: nc.scalar.mul(out=RSTD[:, NP:NT], in_=SM[1][:, 0:NS], mul=1.0 / 1024)],
                 reads=[bPB[0], bPSM[1]], writes=[bRSTD])
            S.op("dve", lambda: nc.vector.tensor_tensor(out=MSQ[:], in0=MEAN[:], in1=MEAN[:], op=ALU.mult), reads=[bTMP[0]], writes=[bTMP[1]])
            S.op("dve", lambda: nc.vector.tensor_tensor(out=RSTD[:], in0=RSTD[:], in1=MSQ[:], op=ALU.subtract), reads=[bTMP[1]], writes=[bRSTD])
            S.op("act", lambda: nc.scalar.activation(out=RSTD[:], in_=RSTD[:], func=AF.Sqrt, bias=EPSC[:], scale=1.0), reads=[bCONST], writes=[bRSTD])
            S.op("dve", lambda: nc.vector.reciprocal(out=RSTD[:], in_=RSTD[:]), reads=[], writes=[bRSTD])
            YT = arb(0, 8 * NT).rearrange("p (c t) -> p c t", c=8)
            bYT = newbuf("yt")
            for c in range(8):
                S.op("dve", [lambda c=c: nc.vector.tensor_tensor(out=Y[:, c, :], in0=Y[:, c, :], in1=MEAN[:, 0:NP], op=ALU.subtract),
                             lambda c=c: nc.vector.tensor_tensor(out=YS[:, c, :], in0=YS[:, c, :], in1=MEAN[:, NP:NT], op=ALU.subtract)],
                     reads=[bTMP[0]], writes=[bY[c], bYS])
                S.op("dve", [lambda c=c: nc.vector.tensor_tensor(out=Y[:, c, :], in0=Y[:, c, :], in1=RSTD[:, 0:NP], op=ALU.mult),
                             lambda c=c: nc.vector.tensor_tensor(out=YS[:, c, :], in0=YS[:, c, :], in1=RSTD[:, NP:NT], op=ALU.mult)],
                     reads=[bRSTD], writes=[bY[c], bYS])
                S.op("act", [lambda c=c: nc.scalar.activation(out=YT[:, c, 0:NP], in_=Y[:, c, :], func=AF.Silu, bias=VEC[:, V_LNB + c:V_LNB + c + 1], scale=VEC[:, V_LNG + c:V_LNG + c + 1]),
                             lambda c=c: nc.scalar.activation(out=YT[:, c, NP:NT], in_=YS[:, c, :], func=AF.Silu, bias=VEC[:, V_LNB + c:V_LNB + c + 1], scale=VEC[:, V_LNG + c:V_LNG + c + 1])],
                     reads=[bY[c], bYS, bVEC], writes=[bYT])
            yt_rhs = lambda k, t0, tn: YT[:, k, t0:t0 + tn]
            for n in range(KC):
                j = proj(wout_d[16 + n], yt_rhs, 8, [bYT])
                resid_evac(j, n, 5)
            if stage == 2:
                finish(XT[:], bXT); return

            Q = arf(0, NT); F = arf(1040, NT); V = arf(2080, NT); GATE = arf(3120, NT)
            KHT2 = arb(4160, 1024).rearrange("p (t d) -> p t d", t=8); VT2 = arb(4672, 1024).rearrange("p (t d) -> p t d", t=8)
            QTb = arb(5184, 1024); KTb = arb(5696, 1024)
            SbV = TMP[0][:, 0:512].bitcast(BF16)
            Sb = [SbV[:, i * 128:(i + 1) * 128] for i in range(8)]
            OT = arb(6208, 8 * NT).rearrange("p (c t) -> p c t", c=8)
            SR = [arf(10368 + 128 * i, 128) for i in range(8)]
            RCV = [arf(11392, 1024), arf(12416, 1024)]
            ATT = [arb(13440, 128), arb(13504, 128)]
            bQ = newbuf("q"); bF = newbuf("f"); bV = newbuf("v"); bGT = newbuf("gate"); bKHT2 = newbuf("kht2"); bVT2 = newbuf("vt2"); bQTb = newbuf("qtb"); bKTb = newbuf("ktb"); bSb = [Buf("sb%d" % i) for i in range(8)]
            bOT = newbuf("ot"); bSR = [newbuf("sr%d" % i) for i in range(8)]; bRCV = [newbuf("rcv0"), newbuf("rcv1")]; bATT = [newbuf("att0"), newbuf("att1")]
            CMASK = SQ[0][:, 0:NP]
            S.op("dve", lambda: nc.vector.memset(CMASK, 1.0), reads=[], writes=[bSQ[0]])
            S.op("dve", lambda: nc.vector.memset(CMASK.rearrange("p (c t) -> p c t", t=32)[:, :, 0:1], 0.0), reads=[], writes=[bSQ[0]])
            rcv_slot = S.slot()
            VTOK = RSTD[0:16, 0:128]; KTOK = RSTD[0:16, 128:256]; KM4 = [RSTD[0:16, 256 + 128 * i:384 + 128 * i] for i in range(4)]
            bKM4 = Buf("km4")
            DSb = [PB[0][:, 0:512], PB[1][:, 0:512]]; ATp = [SM[0][:, 0:128], SM[1][:, 0:128]]
            VTm = arb(13568, 512).rearrange("p (c d) -> p c d", c=4); bVTm = newbuf("vtm")
            BM = MASKF.rearrange("p (c t) -> p c t", t=32)[:, :, 31]
            OUT4 = [PB[0][:, 512:1024], PB[1][:, 512:1024]]
            ID16 = VEC[0:16, V_ID:V_ID + 16]
            O = TMP[1]
            nhp_slot = S.slot(); shg_slot = [S.slot(), S.slot()]; nhs_slot = [S.slot(), S.slot()]
            bRCVr = [newbuf("rcvr%d" % i) for i in range(4)]
            POa = [PST[:, 0:128], PST[:, 512:640]]; bPO = [bPSTa, bPSTb]
            ci_all = 0
            for h in range(NH):
                S.dma(SP, rcv_slot, lambda h=h: nc.sync.dma_start(out=SR[0], in_=scr.ap()[:, h * 128:(h + 1) * 128]), reads=[bSCR], writes=[bSR[0]])
                for half in range(2):
                    n0 = half * 8
                    S.dma(SP, shg_slot[half], lambda h=h, n0=n0, half=half: nc.sync.dma_start(out=RCV[half].rearrange("p (n d) -> p n d", n=8), in_=shg_d[:, n0:n0 + 8, h, :]),
                          writes=[bRCV[half], bRCVr[2 * half], bRCVr[2 * half + 1]])
                jq = proj(win_d[h], ht_rhs, KC, [bHT, bHTs])
                S.op("act", [lambda jq=jq, h=h: nc.scalar.activation(out=Q[:, 0:NP], in_=PB[jq][:], func=AF.Silu, bias=bin_col(h), scale=1.0),
                             lambda jq=jq, h=h: nc.scalar.activation(out=Q[:, NP:NT], in_=PSMv[jq], func=AF.Silu, bias=bin_col(h), scale=1.0)],
                     reads=[bPB[jq], bPSM[jq], bVEC], writes=[bQ])
                mod_job(); mod_job()
                jf = proj(win_d[8 + h], ht_rhs, KC, [bHT, bHTs])
                mod_job(); mod_job()
                S.op("act", [lambda jf=jf, h=h: nc.scalar.activation(out=F[:, 0:NP], in_=PB[jf][:], func=AF.Sigmoid, bias=bin_col(8 + h), scale=1.0),
                             lambda jf=jf, h=h: nc.scalar.activation(out=F[:, NP:NT], in_=PSMv[jf], func=AF.Sigmoid, bias=bin_col(8 + h), scale=1.0)],
                     reads=[bPB[jf], bPSM[jf], bVEC], writes=[bF])
                S.op("dve", lambda h=h: nc.vector.tensor_scalar(out=F, in0=F, scalar1=OML[:, h:h + 1], scalar2=LBv[:, h:h + 1], op0=ALU.mult, op1=ALU.add),
                     reads=[bSMT], writes=[bF])
                ji = proj(win_d[16 + h], ht_rhs, KC, [bHT, bHTs])
                mod_job()
                S.op("act", [lambda ji=ji, h=h: nc.scalar.activation(out=V[:, 0:NP], in_=PB[ji][:], func=AF.Identity, bias=bin_col(16 + h), scale=1.0),
                             lambda ji=ji, h=h: nc.scalar.activation(out=V[:, NP:NT], in_=PSMv[ji], func=AF.Identity, bias=bin_col(16 + h), scale=1.0)],
                     reads=[bPB[ji], bPSM[ji], bVEC], writes=[bV])
                jg = proj(win_d[24 + h], ht_rhs, KC, [bHT, bHTs])
                mod_job()
                S.op("act", [lambda jg=jg, h=h: nc.scalar.activation(out=GATE[:, 0:NP], in_=PB[jg][:], func=AF.Silu, bias=bin_col(24 + h), scale=1.0),
                             lambda jg=jg, h=h: nc.scalar.activation(out=GATE[:, NP:NT], in_=PSMv[jg], func=AF.Silu, bias=bin_col(24 + h), scale=1.0)],
                     reads=[bPB[jg], bPSM[jg], bVEC], writes=[bGT])
                alias_into(bTMP[0], bSb)
                LF = TMP[0][:, 0:NP]; Bc = TMP[1][:, 0:NP]; E = RSTD[:, 0:NP]
                S.op("act", lambda: nc.scalar.activation(out=LF, in_=F[:, 0:NP], func=AF.Ln), reads=[bF], writes=[bTMP[0]])
                S.op("dve", lambda: nc.vector.tensor_tensor_scan(out=Bc, data0=CMASK, data1=LF, initial=0.0, op0=ALU.mult, op1=ALU.add),
                     reads=[bTMP[0], bSQ[0]], writes=[bTMP[1]])
                S.op("act", lambda: nc.scalar.activation(out=E, in_=Bc, func=AF.Exp), reads=[bTMP[1]], writes=[bRSTD])
                S.op("dve", lambda: nc.vector.tensor_tensor(out=QTb, in0=Q[:, 0:NP], in1=E, op=ALU.mult), reads=[bRSTD, bQ], writes=[bQTb])
                S.op("act", lambda: nc.scalar.activation(out=EBLt, in_=Bc.rearrange("p (c t) -> p c t", t=32)[:, :, 31], func=AF.Exp), reads=[bTMP[1]], writes=[bSMT])
                S.op("act", lambda: nc.scalar.activation(out=LF, in_=Bc, func=AF.Exp, scale=-1.0), reads=[bTMP[1]], writes=[bTMP[0]])
                S.op("dve", lambda: nc.vector.tensor_copy(out=FS, in_=F[:, NP:NT]), reads=[bF], writes=[bSMT])
                S.op("dve", lambda: nc.vector.tensor_scalar(out=F, in0=F, scalar1=-1.0, scalar2=1.0, op0=ALU.mult, op1=ALU.add), reads=[], writes=[bF])
                S.op("dve", lambda: nc.vector.tensor_tensor(out=LF, in0=F[:, 0:NP], in1=LF, op=ALU.mult), reads=[bF], writes=[bTMP[0]])
                S.op("dve", lambda: nc.vector.tensor_tensor(out=F[:, 0:NP].rearrange("p (c t) -> p c t", t=32), in0=LF.rearrange("p (c t) -> p c t", t=32),
                                                            in1=EBLt.unsqueeze(2).to_broadcast([128, 32, 32]), op=ALU.mult),
                     reads=[bTMP[0], bSMT], writes=[bF])
                S.op("act", lambda: nc.scalar.copy(out=KTb, in_=LF), reads=[bTMP[0]], writes=[bKTb])
                transposes_to(F, bF, KHT2, bKHT2)
                transposes_to(V, bV, VT2, bVT2)
                for b_ in bSb:
                    alias_into(b_, [bTMP[0]])
                S.op("act", lambda: nc.scalar.copy(out=Sb[0], in_=SR[0]), reads=[bSR[0]], writes=[bSb[0]])
                def pe_ds(t):
                    a_ = t % 2
                    S.op("dve", lambda: nc.vector.tensor_tensor(out=VTm, in0=VT2[:, t, :].unsqueeze(1).to_broadcast([128, 4, 128]),
                                                                in1=BM.unsqueeze(2).to_broadcast([128, 4, 128]), op=ALU.mult),
                         reads=[bVT2, bVEC], writes=[bVTm])
                    fns = [(lambda cc=cc: nc.tensor.matmul(DSb[a_][:, cc * 128:(cc + 1) * 128], lhsT=KHT2[:, t, :], rhs=VTm[:, cc, :],
                                                           start=True, stop=True)) for cc in range(4)]
                    S.op("pe", fns, reads=[bKHT2, bVTm], writes=[bPB[a_]])

                def pe_att(t):
                    a_ = t % 2
                    S.op("pe", lambda: nc.tensor.matmul(ATp[a_], lhsT=KTb[:, t * 128:(t + 1) * 128], rhs=QTb[:, t * 128:(t + 1) * 128], start=True, stop=True),
                         reads=[bKTb, bQTb], writes=[bPSM[a_]])

                def dve_mask(t):
                    a_ = t % 2
                    S.op("dve", lambda: nc.vector.tensor_tensor(out=ATT[a_], in0=ATp[a_], in1=MASKF, op=ALU.mult), reads=[bPSM[a_], bVEC], writes=[bATT[a_]])

                def dve_chain(t):
                    for cc in range(4):
                        ci = 4 * t + cc
                        S.op("dve", lambda cc=cc, ci=ci: nc.vector.scalar_tensor_tensor(out=SR[(ci + 1) % 8], in0=SR[ci % 8], scalar=EBLt[:, ci:ci + 1],
                                                                                       in1=DSb[t % 2][:, cc * 128:(cc + 1) * 128], op0=ALU.mult, op1=ALU.add),
                             reads=[bSR[ci % 8], bPB[t % 2], bSMT], writes=[bSR[(ci + 1) % 8]])
                        S.op("act", lambda ci=ci: nc.scalar.copy(out=Sb[(ci + 1) % 8], in_=SR[(ci + 1) % 8]), reads=[bSR[(ci + 1) % 8]], writes=[bSb[(ci + 1) % 8]])

                def pe_po(t):
                    a_ = t % 2
                    fns = [lambda: nc.tensor.matmul(POa[a_], lhsT=VT2[:, t, :], rhs=ATT[a_], start=True, stop=False)]
                    for cc in range(4):
                        col = t * 128 + cc * 32
                        fns.append(lambda cc=cc, col=col: nc.tensor.matmul(POa[a_][:, cc * 32:(cc + 1) * 32], lhsT=Sb[(4 * t + cc) % 8], rhs=QTb[:, col:col + 32],
                                                                           start=False, stop=(cc == 3)))
                    S.op("pe", fns, reads=[bVT2, bATT[a_], bQTb] + [bSb[(4 * t + cc) % 8] for cc in range(4)], writes=[bPO[a_]])
                    S.op("act", lambda: nc.scalar.copy(out=O[:, t * 128:(t + 1) * 128], in_=POa[a_]), reads=[bPO[a_]], writes=[bTMP[1]])

                pe_ds(0); pe_att(0); dve_mask(0); dve_chain(0)
                for t in range(8):
                    if t + 1 < 8:
                        pe_ds(t + 1); pe_att(t + 1)
                    pe_po(t)
                    if t + 1 < 8:
                        dve_mask(t + 1); dve_chain(t + 1)
                S.dma(SP, nhp_slot, lambda h=h: nc.sync.dma_start(out=nhp_o[:, h, :], in_=SR[0]), reads=[bSR[0]])
                S.op("pe", [lambda: nc.tensor.transpose(SM[0][0:16, 0:128], V[:, NP:NT], IDF),
                            lambda: nc.tensor.transpose(SM[0][0:16, 128:256], F[:, NP:NT], IDF)], reads=[bV, bF, bVEC], writes=[bPSM[0]])
                S.op("act", lambda: nc.scalar.copy(out=RSTD[0:16, 0:256], in_=SM[0][0:16, 0:256]), reads=[bPSM[0]], writes=[bRSTD])
                for g4 in range(4):
                    a_ = g4 % 2; half = g4 // 2
                    ns_ = [g4 * 4 + i for i in range(4)]
                    sls = [RCV[half][:, (n % 8) * 128:(n % 8 + 1) * 128] for n in ns_]
                    S.op("dve", [(lambda i=i, n=n: nc.vector.tensor_scalar(out=KM4[i], in0=KTOK, scalar1=ID16[:, n:n + 1], scalar2=None, op0=ALU.mult)) for i, n in enumerate(ns_)],
                         reads=[bRSTD, bVEC], writes=[bKM4])
                    S.op("pe", [(lambda i=i, a_=a_: nc.tensor.matmul(OUT4[a_][:, i * 128:(i + 1) * 128], lhsT=KM4[i], rhs=VTOK, start=True, stop=True)) for i in range(4)],
                         reads=[bKM4, bRSTD], writes=[bPB[a_]])
                    S.op("dve", [(lambda i=i, n=n, sl=sl, a_=a_: nc.vector.scalar_tensor_tensor(out=sl, in0=sl, scalar=FS[:, n:n + 1], in1=OUT4[a_][:, i * 128:(i + 1) * 128],
                                                                                            op0=ALU.mult, op1=ALU.add)) for i, (n, sl) in enumerate(zip(ns_, sls))],
                         reads=[bPB[a_], bSMT, bRCV[half]], writes=[bRCVr[g4]])
                    S.op("pe", [(lambda n=n, sl=sl: nc.tensor.matmul(SM[0][:, 256 + n:257 + n], lhsT=sl, rhs=Q[:, NP + n:NP + n + 1], start=True, stop=True)) for n, sl in zip(ns_, sls)],
                         reads=[bRCVr[g4], bQ], writes=[bPSM[0]])
                    if g4 % 2 == 1:
                        S.dma(SP, nhs_slot[half], lambda h=h, half=half: nc.sync.dma_start(out=nhs_o[:, half * 8:half * 8 + 8, h, :], in_=RCV[half].rearrange("p (n d) -> p n d", n=8)),
                              reads=[bRCV[half], bRCVr[2 * half], bRCVr[2 * half + 1]])
                S.op("act", lambda: nc.scalar.copy(out=O[:, NP:NT], in_=SM[0][:, 256:272]), reads=[bPSM[0]], writes=[bTMP[1]])
                S.op("act", lambda: nc.scalar.activation(out=SQ[1][:], in_=O[:], func=AF.Square), reads=[bTMP[1]], writes=[bSQ[1]])
                fns = [(lambda tb=tb: nc.tensor.matmul(PST[:, tb * 512:(tb + 1) * 512], lhsT=ONESB[:], rhs=SQ[1][:, tb * 512:(tb + 1) * 512], start=True, stop=True)) for tb in range(2)]
                fns.append(lambda: nc.tensor.matmul(SM[0][:, 0:NS], lhsT=ONESB[:], rhs=SQ[1][:, NP:NT], start=True, stop=True))
                S.op("pe", fns, reads=[bSQ[1], bCONST], writes=[bPSTa, bPSTb, bPSM[0]])
                S.op("act", [lambda: nc.scalar.activation(out=RSTD[:, 0:NP], in_=PST[:], func=AF.Sqrt, bias=EPSC[:], scale=1.0 / 128),
                             lambda: nc.scalar.activation(out=RSTD[:, NP:NT], in_=SM[0][:, 0:NS], func=AF.Sqrt, bias=EPSC[:], scale=1.0 / 128)],
                     reads=[bPSTa, bPSTb, bPSM[0], bCONST], writes=[bRSTD])
                S.op("dve", lambda: nc.vector.reciprocal(out=RSTD[:], in_=RSTD[:]), reads=[], writes=[bRSTD])
                S.op("dve", lambda: nc.vector.tensor_tensor(out=O[:], in0=O[:], in1=RSTD[:], op=ALU.mult), reads=[bRSTD], writes=[bTMP[1]])
                S.op("dve", lambda h=h: nc.vector.scalar_tensor_tensor(out=OT[:, h, :], in0=O[:], scalar=VEC[:, V_GA + h:V_GA + h + 1], in1=GATE, op0=ALU.mult, op1=ALU.mult),
                     reads=[bTMP[1], bGT, bVEC], writes=[bOT])
            ot_rhs = lambda k, t0, tn: OT[:, k, t0:t0 + tn]
            for n in range(KC):
                j = proj(wout_d[n], ot_rhs, 8, [bOT])
                resid_evac(j, n, 5)
            if stage == 3:
                finish(XT[:], bXT); return

            norm_mod(2)
            for b_ in bG:
                alias_into(b_, arbufs)
            ffn(2, wup2_d, wdn2_d)
            rms_stats()
            for k in range(KC):
                q = k % 2
                S.op("dve", lambda k=k, q=q: nc.vector.tensor_tensor(out=TMP[q][:], in0=XT[:, k, :], in1=RSTD[:], op=ALU.mult), reads=[bXT[k], bRSTD], writes=[bTMP[q]])
                S.op("act", lambda k=k, q=q: nc.scalar.activation(out=XT[:, k, :], in_=TMP[q][:], func=AF.Identity, scale=VEC[:, V_FG + k:V_FG + k + 1]),
                     reads=[bTMP[q], bVEC], writes=[bXT[k]])
            y_slot = S.slot()
            S.dma(SP, y_slot, lambda: nc.sync.dma_start(out=yT_o.rearrange("(k p) t -> p k t", p=128), in_=XT[:]), reads=bXT)
            finish()
            return


        program(DummySched(), True)
        program(S_real, False)
    return nc


def _tiles(W, nk):
    K, N = W.shape
    return np.ascontiguousarray(W.reshape(nk, 128, N // 128, 128).transpose(2, 1, 0, 3)).reshape(N // 128, 128, nk * 128)


def _fm(v):
    return np.ascontiguousarray(np.asarray(v, np.float32).reshape(-1, 128).T)


def prep_inputs(inp):
    f = lambda a: np.asarray(a, np.float32)
    x_prompt = f(inp["x_prompt"]); x_sample = f(inp["x_sample"])[:, 0, :]
    c_prompt = f(inp["c_prompt"]); c_sample = f(inp["c_sample"])
    wada = _tiles(f(inp["w_ada"])[0], 16)

    def up_tiles(w):
        t = _tiles(f(w)[0], 16)
        o = np.empty_like(t)
        o[0::2] = t[:HC]; o[1::2] = t[HC:]
        return o

    def dn_tiles(w):
        w = f(w)[0]
        return np.concatenate([_tiles(w[g * GJ * 128:(g + 1) * GJ * 128], GJ) for g in range(NG)], 0)
    wup1 = up_tiles(inp["w_f1_up"]); wdn1 = dn_tiles(inp["w_f1_down"])
    wup2 = up_tiles(inp["w_f2_up"]); wdn2 = dn_tiles(inp["w_f2_down"])
    win = _tiles(f(inp["w_in"])[0], 16)
    wo = f(inp["w_out"])[0]
    wout = np.concatenate([_tiles(wo[:1024], 8), _tiles(wo[1024:], 8)], 0)
    vec0 = np.zeros((128, NV), np.float32)
    vec0[:, V_BADA:V_BADA + 144] = _fm(f(inp["b_ada"])[0])
    vec0[:, V_BIN:V_BIN + 48] = _fm(f(inp["b_in"])[0])
    vec0[:, V_NG:V_NG + 48] = _fm(f(inp["norm_g"])[0].reshape(-1))
    vec0[:, V_FG:V_FG + 16] = _fm(f(inp["final_g"]))
    vec0[:, V_GA:V_GA + 8] = _fm(f(inp["g_norm_a"])[0])
    vec0[:, V_CB:V_CB + 8] = _fm(f(inp["conv_b"])[0])
    vec0[:, V_LNG:V_LNG + 8] = _fm(f(inp["ln_g"])[0])
    vec0[:, V_LNB:V_LNB + 8] = _fm(f(inp["ln_b"])[0])
    lb = f(inp["lb_logits"])
    vec0[:, V_LB:V_LB + 8] = _fm(lb[0]); vec0[:, V_LB + 8:V_LB + 16] = _fm(lb[1])
    cw = f(inp["conv_w"])[0]
    vec0[:, V_CW:V_CW + 248] = cw.reshape(CW, 8, 128).transpose(2, 1, 0).reshape(128, 248)
    vec0[:, V_ID:V_ID + 128] = np.eye(128, dtype=np.float32)
    s_i = np.arange(128)[:, None]; t_i = np.arange(128)[None, :]
    vec0[:, V_MASK:V_MASK + 128] = ((s_i // 32 == t_i // 32) & (s_i <= t_i)).astype(np.float32)
    sh = f(inp["state_hgrn"])[0]
    sc = f(inp["state_conv"])[0]
    maps = []
    for c in range(N_CORES):
        seq, half = c // 2, c % 2
        xo = x_prompt[seq, half * NP:(half + 1) * NP]
        xs = x_sample[c * NS:(c + 1) * NS]
        xT = np.ascontiguousarray(np.concatenate([xo, xs], 0).T)
        xTp = np.zeros((D, NT), np.float32)
        if half == 1:
            xTp[:, :NP] = x_prompt[seq, 0:NP].T
        cT = np.ascontiguousarray(np.concatenate([c_sample[c * NS:(c + 1) * NS], c_prompt[seq:seq + 1]], 0).T)
        vec = vec0.copy()
        vec[:, V_M] = float(half)
        if half == 0:
            vec[:, V_OH + seq] = 1.0
        else:
            vec[:, V_SEL + seq] = 1.0
        shg = np.ascontiguousarray(sh[c * NS:(c + 1) * NS].transpose(2, 0, 1, 3))
        scv = np.ascontiguousarray(sc[c * NS:(c + 1) * NS].reshape(NS, HIST, 8, 128).transpose(3, 2, 0, 1))
        maps.append({"xT": xT, "xTp": xTp, "cT": cT, "vec": vec, "wada": wada, "wup1": wup1, "wdn1": wdn1, "wup2": wup2,
                     "wdn2": wdn2, "win": win, "wout": wout, "shg": shg, "scv": scv})
    return maps


def kernel(**inputs):
    maps = prep_inputs(inputs)
    nc = build_nc()
    res = run_bass_kernel_spmd(nc, maps, core_ids=list(range(N_CORES)))
    r = res.results
    y_prompt = np.empty((4, 2048, D), np.float32); y_sample = np.empty((128, 1, D), np.float32)
    nhp = np.empty((1, 4, NH, 128, 128), np.float32); nhs = np.empty((1, 128, NH, 128, 128), np.float32)
    ncp = np.empty((1, 4, HIST, 1024), np.float32); ncs = np.empty((1, 128, HIST, 1024), np.float32)
    for c in range(N_CORES):
        seq, half = c // 2, c % 2
        yT = r[c]["yT"]
        y_prompt[seq, half * NP:(half + 1) * NP] = yT[:, :NP].T
        y_sample[c * NS:(c + 1) * NS, 0] = yT[:, NP:].T
        nhs[0, c * NS:(c + 1) * NS] = r[c]["nhs"].transpose(1, 2, 0, 3)
        ncs[0, c * NS:(c + 1) * NS] = r[c]["ncs"].transpose(2, 3, 1, 0).reshape(NS, HIST, 1024)
        if half == 1:
            nhp[0, seq] = r[c]["nhp"].transpose(1, 0, 2)
            ncp[0, seq] = r[c]["ncp"].transpose(2, 1, 0).reshape(HIST, 1024)
    return (y_prompt, y_sample, nhp, nhs, ncp, ncs)
```

```python
import numpy as np
from contextlib import ExitStack
import concourse.bass as bass
import concourse.mybir as mybir
from concourse.bass_utils import run_bass_kernel_spmd

F32 = mybir.dt.float32
BF16 = mybir.dt.bfloat16
ALU = mybir.AluOpType
AF = mybir.ActivationFunctionType

D = 2048; KC = 16; NP = 1024; NS = 16; NT = NP + NS
DFF = 5632; HC = 44; NG = 4; GJ = 11
NH = 8; CW = 31; HIST = 30
EPS = 1e-6
N_CORES = 8
TBS = [(0, 512), (512, 512)]

V_BADA = 0
V_BIN = 144
V_NG = 192
V_FG = 240
V_GA = 256
V_CB = 264
V_LNG = 272
V_LNB = 280
V_LB = 288
V_CW = 304
V_M = 552
V_OH = 553
V_SEL = 557
V_ID = 561
V_MASK = 689
NV = 817

XW = NH * 128 + NH * HIST


class Buf:
    __slots__ = ("name", "w", "r")

    def __init__(s, name):
        s.name = name; s.w = None; s.r = {}


class Slot:
    def __init__(s, sem):
        s.sem = sem; s.val = 0


class Sched:
    def __init__(s, nc, stack):
        s.nc = nc
        s.eng = {"pe": nc.tensor, "act": nc.scalar, "dve": nc.vector, "pool": nc.gpsimd, "sp": nc.sync}
        s.ops = {k: [] for k in s.eng}
        s.sem = {k: stack.enter_context(nc.semaphore("s_" + k)) for k in s.eng}
        s.cnt = {k: 0 for k in s.eng}
        s.waited = {k: {} for k in s.eng}
        s.stack = stack
        s.nslots = 0
        s.slots = []

    def slot(s):
        s.nslots += 1
        sl = Slot(s.stack.enter_context(s.nc.semaphore("d%d" % s.nslots)))
        s.slots.append(sl)
        return sl

    def _wait(s, e, ev):
        if ev is None:
            return
        sem, val, owner = ev
        if owner == e and e == "pe":
            return
        key = id(sem)
        if s.waited[e].get(key, 0) >= val:
            return
        s.waited[e][key] = val
        h = s.eng[e]
        s.ops[e].append(lambda: h.wait_ge(sem, val))

    def _deps(s, e, reads, writes):
        for b in reads:
            s._wait(e, b.w)
        for b in writes:
            s._wait(e, b.w)
            for ev in list(b.r.values()):
                s._wait(e, ev)

    def _post(s, ev, reads, writes):
        for b in reads:
            b.r[id(ev[0])] = ev
        for b in writes:
            b.w = ev; b.r = {}

    def op(s, e, fns, reads=(), writes=()):
        if callable(fns):
            fns = [fns]
        s._deps(e, reads, writes)
        s.cnt[e] += 1
        sem = s.sem[e]; ev = (sem, s.cnt[e], e)
        for f in fns[:-1]:
            s.ops[e].append(f)
        last = fns[-1]
        s.ops[e].append(lambda: last().then_inc(sem, 1))
        s._post(ev, reads, writes)
        return ev

    def dma(s, e, slot, fn, reads=(), writes=(), inc=16):
        s._deps(e, reads, writes)
        slot.val += inc
        sem = slot.sem; ev = (sem, slot.val, "dma")
        if inc == 1:
            s.ops[e].append(lambda: fn().then_inc(sem))
        else:
            s.ops[e].append(lambda: fn().then_inc(sem, inc))
        s._post(ev, reads, writes)
        return ev

    def run(s):
        nc = s.nc
        with nc.Block() as block:
            @block.sync
            def _(e):
                for f in s.ops["sp"]:
                    f()

            @block.scalar
            def _(e):
                for f in s.ops["act"]:
                    f()

            @block.vector
            def _(e):
                for f in s.ops["dve"]:
                    f()

            @block.gpsimd
            def _(e):
                for f in s.ops["pool"]:
                    f()

            @block.tensor
            def _(e):
                for f in s.ops["pe"]:
                    f()


class DummySched:
    def __init__(s):
        s.slots = []

    def slot(s):
        return Slot(None)

    def op(s, *a, **k):
        return None

    def dma(s, *a, **k):
        return None

    def _wait(s, *a, **k):
        return None

    def run(s):
        return None


def build_nc(stage=99, ncores=N_CORES):
    nc = bass.Bass("TRN2", target_bir_lowering=False)
    din = lambda name, shape: nc.dram_tensor(name, shape, F32, kind="ExternalInput").ap()
    dout = lambda name, shape: nc.dram_tensor(name, shape, F32, kind="ExternalOutput").ap()
    xT_d = din("xT", [D, NT])
    xTp_d = din("xTp", [D, NT])
    cT_d = din("cT", [D, 17])
    vec_d = din("vec", [128, NV])
    wada_d = din("wada", [144, 128, 2048])
    wup1_d = din("wup1", [88, 128, 2048]); wdn1_d = din("wdn1", [64, 128, GJ * 128])
    wup2_d = din("wup2", [88, 128, 2048]); wdn2_d = din("wdn2", [64, 128, GJ * 128])
    win_d = din("win", [48, 128, 2048])
    wout_d = din("wout", [32, 128, 1024])
    shg_d = din("shg", [128, NS, NH, 128])
    scv_d = din("scv", [128, NH, NS, HIST])
    yT_o = dout("yT", [D, NT])
    nhp_o = dout("nhp", [128, NH, 128])
    nhs_o = dout("nhs", [128, NS, NH, 128])
    ncp_o = dout("ncp", [128, NH, HIST])
    ncs_o = dout("ncs", [128, NH, NS, HIST])
    scr = nc.dram_tensor("scr", [128, 1024], F32)
    dbg = dout("dbg", [128, KC, NT]) if stage < 99 else None

    with ExitStack() as st:
        S_real = Sched(nc, st)
        sb = lambda name, shape, dt=F32: st.enter_context(nc.sbuf_tensor(name, shape, dt))
        ps = lambda name, shape: st.enter_context(nc.psum_tensor(name, shape, F32))

        XT = sb("XT", [128, KC, NT]); bXT = [Buf("xt%d" % k) for k in range(KC)]
        HT = sb("HT", [128, KC, NT], BF16); bHT = Buf("ht"); bHTs = Buf("hts")
        VEC = sb("VEC", [128, NV]); bVEC = Buf("vec")
        MOD = sb("MOD", [128, 144, 17]); bMODm = [Buf("mod%d" % m) for m in range(9)]
        bCT = Buf("ct")
        SMT = sb("SMT", [128, 128]); bSMT = Buf("smt")
        HR = sb("HR", [128, KC, HIST], BF16); bHR = Buf("hr")
        STG = [sb("STG%d" % i, [128, 2048]) for i in range(2)]; bSTG = [Buf("stg%d" % i) for i in range(2)]
        WB = [sb("WB%d" % i, [128, 2048], BF16) for i in range(2)]; bWB = [Buf("wb%d" % i) for i in range(2)]
        ONESB = sb("ONESB", [128, 128], BF16); IDB = sb("IDB", [128, 128], BF16)
        EPSC = sb("EPSC", [128, 1]); bCONST = Buf("const")
        RSTD = sb("RSTD", [128, NT]); bRSTD = Buf("rstd")
        SQ = [sb("SQ%d" % i, [128, NT], BF16) for i in range(2)]; bSQ = [Buf("sq%d" % i) for i in range(2)]
        TMP = [sb("TMP%d" % i, [128, NT]) for i in range(2)]; bTMP = [Buf("tmp%d" % i) for i in range(2)]
        T16 = [sb("T16_%d" % i, [128, NS]) for i in range(2)]; bT16 = [Buf("t16_%d" % i) for i in range(2)]
        ARW = 13824
        AR = sb("AR", [128, ARW])
        CTf = AR[:, 13000:13272].rearrange("p (k n) -> p k n", n=17)
        CTbt = sb("CTbt", [128, KC, 17], BF16); CTb = CTbt[:]
        G = AR[:, 0:GJ * NT // 2].bitcast(BF16).rearrange("p (j t) -> p j t", j=GJ)
        bG = [Buf("g%d" % j) for j in range(GJ)]
        for b_ in bG:
            pass

        PB = [ps("PB0", [128, 1024]), ps("PB1", [128, 1024])]
        SM = [ps("SM0", [128, 512]), ps("SM1", [128, 512])]
        PST = ps("PST", [128, 1024])
        bPB = [Buf("pb0"), Buf("pb1")]; bPSTa = Buf("psta"); bPSTb = Buf("pstb"); bPSM = [Buf("sm0"), Buf("sm1")]
        PSMv = [SM[0][:, 0:16], SM[1][:, 0:16]]

        plan = []

        def program(S, planning):
            SP = "sp"
            ld = [S.slot() for _ in range(4)]
            stg_slot = [S.slot(), S.slot()]
            out_slots = []

            S.dma(SP, ld[0], lambda: nc.sync.dma_start(out=VEC[:], in_=vec_d), writes=[bVEC])
            S.dma(SP, ld[1], lambda: nc.sync.dma_start(out=CTf[:], in_=cT_d.rearrange("(k p) n -> p k n", p=128)), writes=[bCT])
            S.op("dve", [lambda: nc.vector.memset(ONESB[:], 1.0), lambda: nc.vector.memset(EPSC[:], EPS),
                         lambda: nc.vector.tensor_copy(out=IDB[:], in_=VEC[:, V_ID:V_ID + 128])], reads=[bVEC], writes=[bCONST])
            S.op("act", lambda: nc.scalar.activation(out=CTb[:], in_=CTf[:], func=AF.Silu), reads=[], writes=[bCT])
            for k in range(KC):
                pass

            arbufs = list(bG) + [bCT]

            def merge_ev(b, ev):
                k_ = id(ev[0])
                if k_ not in b.r or b.r[k_][1] < ev[1]:
                    b.r[k_] = ev

            def alias_into(b, olds):
                for o in olds:
                    if o is b:
                        continue
                    if o.w is not None:
                        merge_ev(b, o.w)
                    for ev in o.r.values():
                        merge_ev(b, ev)

            def newbuf(name):
                b = Buf(name)
                alias_into(b, arbufs)
                arbufs.append(b)
                return b

            arf = lambda off, n: AR[:, off:off + n]
            arb = lambda off, nb: AR[:, off:off + nb // 2].bitcast(BF16)

            ws = {"dma": 0, "cast": 0, "use": 0}
            WCLS = {"ffn": False}
            STGS = STG + [AR[:, 5720:7768], AR[:, 7768:9816]]
            bSTGS = bSTG + [newbuf("ars2"), newbuf("ars3")]
            WBS = WB + [AR[:, 9816:10840].bitcast(BF16), AR[:, 10840:11864].bitcast(BF16)]
            bWBS = bWB + [newbuf("arw2"), newbuf("arw3")]
            stg_slots4 = stg_slot + [S.slot(), S.slot()]
            slot_of = {}; wb_of = {}
            last_user = {}; wb_last = {}

            def is_deep(k):
                return all(0 <= k - d < len(plan) and plan[k - d][2] for d in range(0, 9))

            def ws_dma(i):
                ap, w, _ = plan[i]
                n_ = 4 if is_deep(i) else 2
                s_ = min(range(n_), key=lambda c: last_user.get(c, -1))
                slot_of[i] = s_; last_user[s_] = i
                S.dma(SP, stg_slots4[s_], lambda: nc.sync.dma_start(out=STGS[s_][:, 0:w], in_=ap), writes=[bSTGS[s_]])

            def dma_ok(k):
                n_ = 4 if is_deep(k) else 2
                return min(last_user.get(c, -1) for c in range(n_)) < ws["cast"]

            def ws_cast(i):
                ap, w, _ = plan[i]
                s_ = slot_of[i]
                n_ = 4 if is_deep(i) else 2
                wq = min(range(n_), key=lambda c: wb_last.get(c, -1))
                wb_of[i] = wq; wb_last[wq] = i
                if i % 2 == 0:
                    S.op("act", lambda: nc.scalar.copy(out=WBS[wq][:, 0:w], in_=STGS[s_][:, 0:w]), reads=[bSTGS[s_]], writes=[bWBS[wq]])
                else:
                    S.op("dve", lambda: nc.vector.tensor_copy(out=WBS[wq][:, 0:w], in_=STGS[s_][:, 0:w]), reads=[bSTGS[s_]], writes=[bWBS[wq]])

            def cast_ok(k, i):
                n_ = 4 if is_deep(k) else 2
                return min(wb_last.get(c, -1) for c in range(n_)) < i

            def ws_next(ap, w):
                i = ws["use"]
                ws["use"] += 1
                if planning:
                    plan.append((ap, w, WCLS["ffn"]))
                    return WB[i % 2][:, 0:w].rearrange("p (k n) -> p k n", n=128), bWB[i % 2]
                assert plan[i][1] == w
                while ws["cast"] < min(len(plan), i + 4) and (ws["cast"] <= i or cast_ok(ws["cast"], i)):
                    k = ws["cast"]
                    assert cast_ok(k, i)
                    while ws["dma"] <= k:
                        ws_dma(ws["dma"]); ws["dma"] += 1
                    ws_cast(k); ws["cast"] += 1
                while ws["dma"] < min(len(plan), i + 8) and dma_ok(ws["dma"]):
                    ws_dma(ws["dma"]); ws["dma"] += 1
                wq = wb_of[i]
                return WBS[wq][:, 0:w].rearrange("p (k n) -> p k n", n=128), bWBS[wq]

            job_ctr = {"n": 0}
            FL = {"small": True}

            def proj(tile, rhs_fn, nk, rbufs, small=True, big=True, hist=False):
                small = small and FL["small"]
                wv, wbuf = ws_next(tile, nk * 128)
                j = job_ctr["n"] % 2; job_ctr["n"] += 1
                fns = []
                for k in range(nk):
                    if big:
                        for bi, (t0, tn) in enumerate(TBS):
                            fns.append((lambda k=k, bi=bi, t0=t0, tn=tn: nc.tensor.matmul(
                                PB[j][:, bi * 512:(bi + 1) * 512], lhsT=wv[:, k, :], rhs=rhs_fn(k, t0, tn),
                                start=(k == 0), stop=(k == nk - 1))))
                    if small:
                        fns.append((lambda k=k: nc.tensor.matmul(
                            PSMv[j], lhsT=wv[:, k, :], rhs=rhs_fn(k, NP, NS), start=(k == 0), stop=(k == nk - 1))))
                if hist:
                    for k in range(nk):
                        fns.append((lambda k=k: nc.tensor.matmul(
                            SM[j][:, 32:32 + HIST], lhsT=wv[:, k, :], rhs=HR[:, k, :], start=(k == 0), stop=(k == nk - 1))))
                wr = ([bPB[j]] if big else []) + ([bPSM[j]] if (small or hist) else [])
                S.op("pe", fns, reads=[wbuf] + list(rbufs) + ([bHR] if hist else []), writes=wr)
                return j

            ht_rhs = lambda k, t0, tn: HT[:, k, t0:t0 + tn]

            mod_state = {"next": 0}

            def mod_job(limit=144):
                oc = mod_state["next"]
                if oc >= limit:
                    return False
                mod_state["next"] = oc + 1
                m = oc // 16
                wv, wbuf = ws_next(wada_d[oc], 2048)
                fns = [(lambda k=k: nc.tensor.matmul(SM[1][:, 256:273], lhsT=wv[:, k, :], rhs=CTb[:, k, :],
                                                     start=(k == 0), stop=(k == KC - 1))) for k in range(KC)]
                S.op("pe", fns, reads=[wbuf, bCT], writes=[bPSM[1]])
                S.op("act", lambda: nc.scalar.activation(out=MOD[:, oc, :], in_=SM[1][:, 256:273], func=AF.Identity,
                                                         bias=VEC[:, V_BADA + oc:V_BADA + oc + 1], scale=1.0),
                     reads=[bPSM[1], bVEC], writes=[bMODm[m]])
                if oc % 16 == 15:
                    blk = MOD[:, m * 16:(m + 1) * 16, :]
                    if m in (1, 4, 7):
                        i = m // 3
                        ngb = VEC[:, V_NG + i * 16:V_NG + (i + 1) * 16].unsqueeze(2).to_broadcast([128, 16, 17])
                        S.op("dve", lambda: nc.vector.scalar_tensor_tensor(out=blk, in0=blk, scalar=1.0, in1=ngb, op0=ALU.add, op1=ALU.mult),
                             reads=[bVEC], writes=[bMODm[m]])
                    if m in (2, 8):
                        S.op("dve", lambda: nc.vector.tensor_scalar(out=blk, in0=blk, scalar1=0.5, scalar2=None, op0=ALU.mult),
                             reads=[], writes=[bMODm[m]])
                return True

            WCLS["ffn"] = True
            for _ in range(32):
                mod_job()
            WCLS["ffn"] = False
            PC = 16
            mod_p = lambda m, k: MOD[:, m * 16 + k, PC:PC + 1]
            mod_s = lambda m, k: MOD[:, m * 16 + k, 0:NS]

            def rms_stats():
                for k in range(KC):
                    q = k % 2
                    W_ = NT if FL["small"] else NP
                    S.op("act", lambda k=k, q=q, W_=W_: nc.scalar.activation(out=SQ[q][:, 0:W_], in_=XT[:, k, 0:W_], func=AF.Square),
                         reads=[bXT[k]], writes=[bSQ[q]])
                    fns = [(lambda q=q, bi=bi, t0=t0, tn=tn, k=k: nc.tensor.matmul(
                        PST[:, bi * 512:(bi + 1) * 512], lhsT=ONESB[:], rhs=SQ[q][:, t0:t0 + tn], start=(k == 0), stop=(k == KC - 1)))
                        for bi, (t0, tn) in enumerate(TBS)]
                    if FL["small"]:
                        fns.append(lambda q=q, k=k: nc.tensor.matmul(SM[0][:, 0:NS], lhsT=ONESB[:], rhs=SQ[q][:, NP:NT],
                                                                     start=(k == 0), stop=(k == KC - 1)))
                    S.op("pe", fns, reads=[bSQ[q], bCONST], writes=[bPSTa, bPSTb, bPSM[0]])
                fa = [lambda: nc.scalar.activation(out=RSTD[:, 0:NP], in_=PST[:], func=AF.Sqrt, bias=EPSC[:], scale=1.0 / D)]
                if FL["small"]:
                    fa.append(lambda: nc.scalar.activation(out=RSTD[:, NP:NT], in_=SM[0][:, 0:NS], func=AF.Sqrt, bias=EPSC[:], scale=1.0 / D))
                S.op("act", fa, reads=[bPSTa, bPSTb, bPSM[0], bCONST], writes=[bRSTD])
                W_ = NT if FL["small"] else NP
                S.op("dve", lambda: nc.vector.reciprocal(out=RSTD[:, 0:W_], in_=RSTD[:, 0:W_]), reads=[], writes=[bRSTD])

            def norm_mod(i):
                rms_stats()
                for k in range(KC):
                    q = k % 2
                    W_ = NT if FL["small"] else NP
                    S.op("dve", lambda k=k, q=q, W_=W_: nc.vector.tensor_tensor(out=TMP[q][:, 0:W_], in0=XT[:, k, 0:W_], in1=RSTD[:, 0:W_], op=ALU.mult),
                         reads=[bXT[k], bRSTD], writes=[bTMP[q]])
                    S.op("act", lambda k=k, q=q: nc.scalar.activation(out=HT[:, k, 0:NP], in_=TMP[q][:, 0:NP], func=AF.Identity,
                                                                      bias=mod_p(3 * i, k), scale=mod_p(3 * i + 1, k)),
                         reads=[bTMP[q], bMODm[3 * i], bMODm[3 * i + 1]], writes=[bHT])
                    if not FL["small"]:
                        continue
                    S.op("dve", lambda k=k, q=q: nc.vector.tensor_tensor(out=T16[q][:], in0=TMP[q][:, NP:NT], in1=mod_s(3 * i + 1, k), op=ALU.mult),
                         reads=[bTMP[q], bMODm[3 * i + 1]], writes=[bT16[q]])
                    S.op("dve", lambda k=k, q=q: nc.vector.tensor_tensor(out=HT[:, k, NP:NT], in0=T16[q][:], in1=mod_s(3 * i, k), op=ALU.add),
                         reads=[bT16[q], bMODm[3 * i]], writes=[bHTs])

            def resid_evac(j, n, m):
                if not FL["small"]:
                    S.op("dve", lambda: nc.vector.scalar_tensor_tensor(out=XT[:, n, 0:NP], in0=PB[j][:], scalar=mod_p(m, n),
                                                                       in1=XT[:, n, 0:NP], op0=ALU.mult, op1=ALU.add),
                         reads=[bPB[j], bMODm[m]], writes=[bXT[n]])
                    return
                S.op("dve", [lambda: nc.vector.scalar_tensor_tensor(out=XT[:, n, 0:NP], in0=PB[j][:], scalar=mod_p(m, n),
                                                                    in1=XT[:, n, 0:NP], op0=ALU.mult, op1=ALU.add),
                             lambda: nc.vector.tensor_tensor(out=T16[j][:], in0=PSMv[j], in1=mod_s(m, n), op=ALU.mult)],
                     reads=[bPB[j], bPSM[j], bMODm[m]], writes=[bXT[n], bT16[j]])
                S.op("dve", lambda: nc.vector.tensor_tensor(out=XT[:, n, NP:NT], in0=XT[:, n, NP:NT], in1=T16[j][:], op=ALU.add),
                     reads=[bT16[j]], writes=[bXT[n]])

            def ffn(i, wu, wd, bg=None):
                m_gate = 3 * i + 2
                WCLS["ffn"] = True
                for b_ in list(bG) + bSTGS[2:] + bWBS[2:]:
                    alias_into(b_, arbufs)
                for g in range(NG):
                    for jj in range(GJ):
                        hc = g * GJ + jj
                        ja = proj(wu[2 * hc], ht_rhs, KC, [bHT, bHTs])
                        fa = [lambda ja=ja: nc.scalar.activation(out=TMP[ja][:, 0:NP], in_=PB[ja][:], func=AF.Silu)]
                        if FL["small"]:
                            fa.append(lambda ja=ja: nc.scalar.activation(out=TMP[ja][:, NP:NT], in_=PSMv[ja], func=AF.Silu))
                        S.op("act", fa, reads=[bPB[ja], bPSM[ja]], writes=[bTMP[ja]])
                        if bg is not None:
                            bg()
                        jb = proj(wu[2 * hc + 1], ht_rhs, KC, [bHT, bHTs])
                        fd = [lambda ja=ja, jb=jb, jj=jj: nc.vector.tensor_tensor(out=G[:, jj, 0:NP], in0=PB[jb][:], in1=TMP[ja][:, 0:NP], op=ALU.mult)]
                        if FL["small"]:
                            fd.append(lambda ja=ja, jb=jb, jj=jj: nc.vector.tensor_tensor(out=G[:, jj, NP:NT], in0=PSMv[jb], in1=TMP[ja][:, NP:NT], op=ALU.mult))
                        S.op("dve", fd, reads=[bPB[jb], bPSM[jb], bTMP[ja]], writes=[bG[jj]])
                        if bg is not None:
                            bg()
                    g_rhs = lambda k, t0, tn: G[:, k, t0:t0 + tn]
                    for n in range(KC):
                        j = proj(wd[g * KC + n], g_rhs, GJ, bG)
                        resid_evac(j, n, m_gate)
                WCLS["ffn"] = False

            def finish(src_ap=None, rbufs=()):
                osl = S.slot()
                if src_ap is not None:
                    S.dma(SP, osl, lambda: nc.sync.dma_start(out=dbg, in_=src_ap), reads=list(rbufs))
                if planning:
                    return
                for sl in S.slots:
                    if sl.val:
                        S._wait(SP, (sl.sem, sl.val, "dma"))
                S.run()

            S.dma(SP, ld[3], lambda: nc.sync.dma_start(out=XT[:], in_=xTp_d.rearrange("(k p) t -> p k t", p=128)), writes=bXT)
            FL["small"] = False
            norm_mod(0)
            if stage == 0:
                S.op("dve", lambda: nc.vector.tensor_copy(out=XT[:], in_=HT[:]), reads=[bHT, bHTs], writes=bXT)
                finish(XT[:], bXT); return
            ffn(0, wup1_d, wdn1_d, bg=lambda: mod_job(80))
            WCLS["ffn"] = True
            while mod_job(80):
                pass
            WCLS["ffn"] = False
            norm_mod(1)
            S.op("dve", lambda: nc.vector.tensor_copy(out=HR[:], in_=HT[:, :, NP - HIST:NP]), reads=[bHT], writes=[bHR])
            IDF = VEC[:, V_ID:V_ID + 128]
            MASKF = VEC[:, V_MASK:V_MASK + 128]
            bin_col = lambda oc: VEC[:, V_BIN + oc:V_BIN + oc + 1]
            LBv = SMT[:, 0:8]; OML = SMT[:, 8:16]; BEND = SMT[:, 16:17]; FS = SMT[:, 17:33]; EBLt = SMT[:, 33:65]
            S.op("dve", lambda: nc.vector.tensor_tensor(out=LBv, in0=VEC[:, V_LB:V_LB + 8], in1=VEC[:, V_LB + 8:V_LB + 16], op=ALU.subtract),
                 reads=[bVEC], writes=[bSMT])
            S.op("act", lambda: nc.scalar.activation(out=LBv, in_=LBv, func=AF.Sigmoid), reads=[], writes=[bSMT])
            S.op("dve", lambda: nc.vector.tensor_scalar(out=OML, in0=LBv, scalar1=-1.0, scalar2=1.0, op0=ALU.mult, op1=ALU.add),
                 reads=[], writes=[bSMT])

            Fb = arf(4224, 1024); Vb = arf(5248, 1024)
            KHT = arf(6272, 1024).rearrange("p (t d) -> p t d", t=8); VT = arf(7296, 1024).rearrange("p (t d) -> p t d", t=8)
            SU = arf(8320, XW); SEND = SU[:, 0:1024].rearrange("p (h d) -> p h d", h=8)
            bFb = newbuf("fb"); bVb = newbuf("vb"); bKHT = newbuf("kht"); bVT = newbuf("vt"); bSU = newbuf("su")
            ONES1K = SQ[1][:, 0:NP]
            S.op("dve", lambda: nc.vector.memset(ONES1K, 1.0), reads=[], writes=[bSQ[1]])

            def transposes_to(src, bsrc, dst, bdst):
                fns = [(lambda t=t: nc.tensor.transpose(PST[:, t * 128:(t + 1) * 128], src[:, t * 128:(t + 1) * 128], IDF)) for t in range(8)]
                S.op("pe", fns, reads=[bsrc, bVEC], writes=[bPSTa, bPSTb])
                S.op("act", lambda: nc.scalar.copy(out=dst.rearrange("p t d -> p (t d)"), in_=PST[:]), reads=[bPSTa, bPSTb], writes=[bdst])

            for h in range(NH):
                jf = proj(win_d[8 + h], ht_rhs, KC, [bHT], small=False)
                S.op("act", lambda jf=jf, h=h: nc.scalar.activation(out=Fb, in_=PB[jf][:], func=AF.Sigmoid, bias=bin_col(8 + h), scale=1.0),
                     reads=[bPB[jf], bVEC], writes=[bFb])
                S.op("dve", lambda h=h: nc.vector.tensor_scalar(out=Fb, in0=Fb, scalar1=OML[:, h:h + 1], scalar2=LBv[:, h:h + 1], op0=ALU.mult, op1=ALU.add),
                     reads=[bSMT], writes=[bFb])
                ji = proj(win_d[16 + h], ht_rhs, KC, [bHT], small=False)
                S.op("act", lambda ji=ji, h=h: nc.scalar.activation(out=Vb, in_=PB[ji][:], func=AF.Identity, bias=bin_col(16 + h), scale=1.0),
                     reads=[bPB[ji], bVEC], writes=[bVb])
                S.op("act", lambda: nc.scalar.activation(out=TMP[0][:, 0:NP], in_=Fb, func=AF.Ln), reads=[bFb], writes=[bTMP[0]])
                S.op("dve", lambda: nc.vector.tensor_tensor_scan(out=TMP[1][:, 0:NP], data0=ONES1K, data1=TMP[0][:, 0:NP], initial=0.0, op0=ALU.mult, op1=ALU.add),
                     reads=[bTMP[0], bSQ[1]], writes=[bTMP[1]])
                S.op("dve", lambda: nc.vector.tensor_copy(out=BEND, in_=TMP[1][:, NP - 1:NP]), reads=[bTMP[1]], writes=[bSMT])
                S.op("dve", lambda: nc.vector.tensor_scalar(out=TMP[1][:, 0:NP], in0=TMP[1][:, 0:NP], scalar1=BEND, scalar2=-1.0, op0=ALU.subtract, op1=ALU.mult),
                     reads=[bSMT], writes=[bTMP[1]])
                S.op("act", lambda: nc.scalar.activation(out=RSTD[:, 0:NP], in_=TMP[1][:, 0:NP], func=AF.Exp), reads=[bTMP[1]], writes=[bRSTD])
                S.op("dve", lambda: nc.vector.tensor_scalar(out=Fb, in0=Fb, scalar1=-1.0, scalar2=1.0, op0=ALU.mult, op1=ALU.add), reads=[], writes=[bFb])
                S.op("dve", lambda: nc.vector.tensor_tensor(out=Fb, in0=Fb, in1=RSTD[:, 0:NP], op=ALU.mult), reads=[bRSTD], writes=[bFb])
                transposes_to(Fb, bFb, KHT, bKHT)
                transposes_to(Vb, bVb, VT, bVT)
                fns = [(lambda t=t: nc.tensor.matmul(SM[0][:, 0:128], lhsT=KHT[:, t, :], rhs=VT[:, t, :], start=(t == 0), stop=(t == 7))) for t in range(8)]
                S.op("pe", fns, reads=[bKHT, bVT], writes=[bPSM[0]])
                S.op("act", lambda h=h: nc.scalar.copy(out=SEND[:, h, :], in_=SM[0][:, 0:128]), reads=[bPSM[0]], writes=[bSU])
            S.op("dve", lambda: nc.vector.tensor_scalar(out=SU[:, 0:1024], in0=SU[:, 0:1024], scalar1=VEC[:, V_M:V_M + 1], scalar2=None, op0=ALU.mult),
                 reads=[bVEC], writes=[bSU])
            scr_slot = S.slot(); bSCR = Buf("scr")
            S.dma(SP, scr_slot, lambda: nc.sync.dma_start(out=scr.ap(), in_=SU[:, 0:1024]), reads=[bSU], writes=[bSCR])

            FL["small"] = True
            S.dma(SP, ld[2], lambda: nc.sync.dma_start(out=XT[:], in_=xT_d.rearrange("(k p) t -> p k t", p=128)), writes=bXT)
            norm_mod(0)
            for b_ in bG:
                alias_into(b_, arbufs)
            ffn(0, wup1_d, wdn1_d)
            if stage == 1:
                finish(XT[:], bXT); return
            norm_mod(1)

            UTb = arb(0, 8 * 1056).rearrange("p (c t) -> p c t", c=8)
            Y = arf(4224, 8192).rearrange("p (c t) -> p c t", c=8)
            O_SP = 12416
            UTAIL = arf(O_SP, 240).rearrange("p (c t) -> p c t", c=8)
            USs = arf(O_SP + 240, 128).rearrange("p (c t) -> p c t", c=8)
            YS = arf(O_SP + 368, 128).rearrange("p (c t) -> p c t", c=8)
            DIAG = [arb(O_SP + 496 + 64 * i, 128) for i in range(2)]
            SCc = arf(O_SP + 624, 496).rearrange("p (n j) -> p n j", j=31)
            bUTB = newbuf("utb"); bY = [newbuf("y%d" % c) for c in range(8)]; bSML = newbuf("sml"); bYS = newbuf("ys")
            bDIAG = [newbuf("diag0"), newbuf("diag1")]; bSCc = newbuf("scc")
            sc_slot = S.slot(); ncs_slot = S.slot()
            PROD = RSTD[:, 0:496].rearrange("p (n j) -> p n j", j=31)
            HA = SMT[:, 65:65 + HIST]; HB = SMT[:, 96:96 + HIST]; bHAB = Buf("hab")
            for c in range(8):
                ja = proj(win_d[32 + c], ht_rhs, KC, [bHT, bHTs], hist=True)
                S.op("act", [lambda ja=ja, c=c: nc.scalar.activation(out=TMP[0][:, 0:NP], in_=PB[ja][:], func=AF.Identity, bias=bin_col(32 + c), scale=1.0),
                             lambda ja=ja, c=c: nc.scalar.activation(out=TMP[0][:, NP:NT], in_=PSMv[ja], func=AF.Identity, bias=bin_col(32 + c), scale=1.0),
                             lambda ja=ja, c=c: nc.scalar.activation(out=HA, in_=SM[ja][:, 32:32 + HIST], func=AF.Identity, bias=bin_col(32 + c), scale=1.0)],
                     reads=[bPB[ja], bPSM[ja], bVEC], writes=[bTMP[0], bHAB])
                mod_job(96)
                jb = proj(win_d[40 + c], ht_rhs, KC, [bHT, bHTs], hist=True)
                mod_job(96)
                S.op("act", [lambda jb=jb, c=c: nc.scalar.activation(out=TMP[1][:, 0:NP], in_=PB[jb][:], func=AF.Sigmoid, bias=bin_col(40 + c), scale=1.0),
                             lambda jb=jb, c=c: nc.scalar.activation(out=TMP[1][:, NP:NT], in_=PSMv[jb], func=AF.Sigmoid, bias=bin_col(40 + c), scale=1.0),
                             lambda jb=jb, c=c: nc.scalar.activation(out=HB, in_=SM[jb][:, 32:32 + HIST], func=AF.Sigmoid, bias=bin_col(40 + c), scale=1.0)],
                     reads=[bPB[jb], bPSM[jb], bVEC], writes=[bTMP[1], bHAB])
                S.op("dve", [lambda c=c: nc.vector.tensor_tensor(out=UTb[:, c, HIST:HIST + NP], in0=TMP[0][:, 0:NP], in1=TMP[1][:, 0:NP], op=ALU.mult),
                             lambda c=c: nc.vector.tensor_tensor(out=USs[:, c, :], in0=TMP[0][:, NP:NT], in1=TMP[1][:, NP:NT], op=ALU.mult),
                             lambda c=c: nc.vector.tensor_tensor(out=UTAIL[:, c, :], in0=TMP[0][:, NP - HIST:NP], in1=TMP[1][:, NP - HIST:NP], op=ALU.mult),
                             lambda c=c: nc.vector.scalar_tensor_tensor(out=UTb[:, c, 0:HIST], in0=HA, scalar=VEC[:, V_M:V_M + 1], in1=HB, op0=ALU.mult, op1=ALU.mult)],
                     reads=[bTMP[0], bTMP[1], bHAB, bVEC], writes=[bUTB, bSML])
                S.dma(SP, sc_slot, lambda c=c: nc.sync.dma_start(out=SCc[:, :, 0:HIST], in_=scv_d[:, c, :, :]), writes=[bSCc])
                S.op("dve", lambda c=c: nc.vector.tensor_copy(out=SCc[:, :, HIST:HIST + 1], in_=USs[:, c, :].unsqueeze(2)), reads=[bSML], writes=[bSCc])
                cwb = VEC[:, V_CW + c * CW:V_CW + (c + 1) * CW].unsqueeze(1).to_broadcast([128, NS, CW])
                S.op("dve", lambda c=c, cwb=cwb: nc.vector.tensor_tensor(out=PROD, in0=SCc, in1=cwb, op=ALU.mult), reads=[bSCc, bVEC], writes=[bRSTD])
                S.op("dve", lambda c=c: nc.vector.tensor_reduce(out=YS[:, c, :], in_=PROD, axis=mybir.AxisListType.X, op=ALU.add), reads=[bRSTD], writes=[bYS])
                S.op("dve", lambda c=c: nc.vector.tensor_scalar(out=YS[:, c, :], in0=YS[:, c, :], scalar1=VEC[:, V_CB + c:V_CB + c + 1], scalar2=None, op0=ALU.add),
                     reads=[bVEC], writes=[bYS])
                S.dma(SP, ncs_slot, lambda c=c: nc.sync.dma_start(out=ncs_o[:, c, :, :], in_=SCc[:, :, 1:CW]), reads=[bSCc])
            S.dma(SP, ncs_slot, lambda: nc.sync.dma_start(out=ncp_o, in_=UTAIL), reads=[bSML])

            for c in range(8):
                sl_ = c % 2
                for j in range(CW):
                    dq = j % 2
                    S.op("dve", lambda c=c, j=j, dq=dq: nc.vector.tensor_scalar(out=DIAG[dq], in0=IDB[:], scalar1=VEC[:, V_CW + c * CW + j:V_CW + c * CW + j + 1],
                                                                               scalar2=None, op0=ALU.mult), reads=[bVEC, bCONST], writes=[bDIAG[dq]])
                    fns = [(lambda c=c, j=j, dq=dq, tb=tb, sl_=sl_: nc.tensor.matmul(PB[sl_][:, tb * 512:(tb + 1) * 512], lhsT=DIAG[dq],
                                                                                 rhs=UTb[:, c, j + tb * 512:j + tb * 512 + 512],
                                                                                 start=(j == 0), stop=(j == CW - 1))) for tb in range(2)]
                    S.op("pe", fns, reads=[bDIAG[dq], bUTB], writes=[bPB[sl_]])
                S.op("act", lambda c=c, sl_=sl_: nc.scalar.activation(out=Y[:, c, :], in_=PB[sl_][:], func=AF.Identity, bias=VEC[:, V_CB + c:V_CB + c + 1], scale=1.0),
                     reads=[bPB[sl_], bVEC], writes=[bY[c]])

            for c in range(8):
                S.op("act", [lambda c=c: nc.scalar.copy(out=SQ[0][:, 0:NP], in_=Y[:, c, :]),
                             lambda c=c: nc.scalar.copy(out=SQ[0][:, NP:NT], in_=YS[:, c, :])], reads=[bY[c], bYS], writes=[bSQ[0]])
                S.op("act", [lambda c=c: nc.scalar.activation(out=SQ[1][:, 0:NP], in_=Y[:, c, :], func=AF.Square),
                             lambda c=c: nc.scalar.activation(out=SQ[1][:, NP:NT], in_=YS[:, c, :], func=AF.Square)], reads=[bY[c], bYS], writes=[bSQ[1]])
                fns = []
                for tb in range(2):
                    fns.append(lambda c=c, tb=tb: nc.tensor.matmul(PST[:, tb * 512:(tb + 1) * 512], lhsT=ONESB[:], rhs=SQ[0][:, tb * 512:(tb + 1) * 512], start=(c == 0), stop=(c == 7)))
                    fns.append(lambda c=c, tb=tb: nc.tensor.matmul(PB[0][:, tb * 512:(tb + 1) * 512], lhsT=ONESB[:], rhs=SQ[1][:, tb * 512:(tb + 1) * 512], start=(c == 0), stop=(c == 7)))
                fns.append(lambda c=c: nc.tensor.matmul(SM[0][:, 0:NS], lhsT=ONESB[:], rhs=SQ[0][:, NP:NT], start=(c == 0), stop=(c == 7)))
                fns.append(lambda c=c: nc.tensor.matmul(SM[1][:, 0:NS], lhsT=ONESB[:], rhs=SQ[1][:, NP:NT], start=(c == 0), stop=(c == 7)))
                S.op("pe", fns, reads=[bSQ[0], bSQ[1], bCONST], writes=[bPSTa, bPSTb, bPB[0], bPSM[0], bPSM[1]])
            MEAN = TMP[0]; MSQ = TMP[1]
            S.op("act", [lambda: nc.scalar.mul(out=MEAN[:, 0:NP], in_=PST[:], mul=1.0 / 1024), lambda: nc.scalar.mul(out=MEAN[:, NP:NT], in_=SM[0][:, 0:NS], mul=1.0 / 1024)],
                 reads=[bPSTa, bPSTb, bPSM[0]], writes=[bTMP[0]])
            S.op("act", [lambda: nc.scalar.mul(out=RSTD[:, 0:NP], in_=PB[0][:], mul=1.0 / 1024), lambda: nc.scalar.mul(out=RSTD[:, NP:NT], in_=SM[1][:, 0:NS], mul=1.0 / 1024)],
                 reads=[bPB[0], bPSM[1]], writes=[bRSTD])
            S.op("dve", lambda: nc.vector.tensor_tensor(out=MSQ[:], in0=MEAN[:], in1=MEAN[:], op=ALU.mult), reads=[bTMP[0]], writes=[bTMP[1]])
            S.op("dve", lambda: nc.vector.tensor_tensor(out=RSTD[:], in0=RSTD[:], in1=MSQ[:], op=ALU.subtract), reads=[bTMP[1]], writes=[bRSTD])
            S.op("act", lambda: nc.scalar.activation(out=RSTD[:], in_=RSTD[:], func=AF.Sqrt, bias=EPSC[:], scale=1.0), reads=[bCONST], writes=[bRSTD])
            S.op("dve", lambda: nc.vector.reciprocal(out=RSTD[:], in_=RSTD[:]), reads=[], writes=[bRSTD])
            YT = arb(0, 8 * NT).rearrange("p (c t) -> p c t", c=8)
            bYT = newbuf("yt")
            for c in range(8):
                S.op("dve", [lambda c=c: nc.vector.tensor_tensor(out=Y[:, c, :], in0=Y[:, c, :], in1=MEAN[:, 0:NP], op=ALU.subtract),
                             lambda c=c: nc.vector.tensor_tensor(out=YS[:, c, :], in0=YS[:, c, :], in1=MEAN[:, NP:NT], op=ALU.subtract)],
                     reads=[bTMP[0]], writes=[bY[c], bYS])
                S.op("dve", [lambda c=c: nc.vector.tensor_tensor(out=Y[:, c, :], in0=Y[:, c, :], in1=RSTD[:, 0:NP], op=ALU.mult),
                             lambda c=c: nc.vector.tensor_tensor(out=YS[:, c, :], in0=YS[:, c, :], in1=RSTD[:, NP:NT], op=ALU.mult)],
                     reads=[bRSTD], writes=[bY[c], bYS])
                S.op("act", [lambda c=c: nc.scalar.activation(out=YT[:, c, 0:NP], in_=Y[:, c, :], func=AF.Silu, bias=VEC[:, V_LNB + c:V_LNB + c + 1], scale=VEC[:, V_LNG + c:V_LNG + c + 1]),
                             lambda c=c: nc.scalar.activation(out=YT[:, c, NP:NT], in_=YS[:, c, :], func=AF.Silu, bias=VEC[:, V_LNB + c:V_LNB + c + 1], scale=VEC[:, V_LNG + c:V_LNG + c + 1])],
                     reads=[bY[c], bYS, bVEC], writes=[bYT])
            yt_rhs = lambda k, t0, tn: YT[:, k, t0:t0 + tn]
            for n in range(KC):
                j = proj(wout_d[16 + n], yt_rhs, 8, [bYT])
                resid_evac(j, n, 5)
            if stage == 2:
                finish(XT[:], bXT); return

            Q = arf(0, NT); F = arf(1040, NT); V = arf(2080, NT); GATE = arf(3120, NT)
            KHT2 = arb(4160, 1024).rearrange("p (t d) -> p t d", t=8); VT2 = arb(4672, 1024).rearrange("p (t d) -> p t d", t=8)
            QTb = arb(5184, 1024); KTb = arb(5696, 1024)
            SbV = TMP[0][:, 0:512].bitcast(BF16)
            Sb = [SbV[:, i * 128:(i + 1) * 128] for i in range(8)]
            OT = arb(6208, 8 * NT).rearrange("p (c t) -> p c t", c=8)
            SR = [arf(10368 + 128 * i, 128) for i in range(8)]
            RCV = [arf(11392, 1024), arf(12416, 1024)]
            ATT = [arb(13440, 128), arb(13504, 128)]
            bQ = newbuf("q"); bF = newbuf("f"); bV = newbuf("v"); bGT = newbuf("gate"); bKHT2 = newbuf("kht2"); bVT2 = newbuf("vt2"); bQTb = newbuf("qtb"); bKTb = newbuf("ktb"); bSb = [Buf("sb%d" % i) for i in range(8)]
            bOT = newbuf("ot"); bSR = [newbuf("sr%d" % i) for i in range(8)]; bRCV = [newbuf("rcv0"), newbuf("rcv1")]; bATT = [newbuf("att0"), newbuf("att1")]
            CMASK = SQ[0][:, 0:NP]
            S.op("dve", lambda: nc.vector.memset(CMASK, 1.0), reads=[], writes=[bSQ[0]])
            S.op("dve", lambda: nc.vector.memset(CMASK.rearrange("p (c t) -> p c t", t=32)[:, :, 0:1], 0.0), reads=[], writes=[bSQ[0]])
            rcv_slot = S.slot()
            VTOK = RSTD[0:16, 0:128]; KTOK = RSTD[0:16, 128:256]; KM4 = [RSTD[0:16, 256 + 128 * i:384 + 128 * i] for i in range(4)]
            bKM4 = Buf("km4")
            DSb = [PB[0][:, 0:512], PB[1][:, 0:512]]; ATp = [SM[0][:, 0:128], SM[1][:, 0:128]]
            VTm = arb(13568, 512).rearrange("p (c d) -> p c d", c=4); bVTm = newbuf("vtm")
            BM = MASKF.rearrange("p (c t) -> p c t", t=32)[:, :, 31]
            OUT4 = [PB[0][:, 512:1024], PB[1][:, 512:1024]]
            ID16 = VEC[0:16, V_ID:V_ID + 16]
            O = TMP[1]
            nhp_slot = S.slot(); shg_slot = [S.slot(), S.slot()]; nhs_slot = [S.slot(), S.slot()]
            bRCVr = [newbuf("rcvr%d" % i) for i in range(4)]
            POa = [PST[:, 0:128], PST[:, 512:640]]; bPO = [bPSTa, bPSTb]
            ci_all = 0
            for h in range(NH):
                S.dma(SP, rcv_slot, lambda h=h: nc.sync.dma_start(out=SR[0], in_=scr.ap()[:, h * 128:(h + 1) * 128]), reads=[bSCR], writes=[bSR[0]])
                for half in range(2):
                    n0 = half * 8
                    S.dma(SP, shg_slot[half], lambda h=h, n0=n0, half=half: nc.sync.dma_start(out=RCV[half].rearrange("p (n d) -> p n d", n=8), in_=shg_d[:, n0:n0 + 8, h, :]),
                          writes=[bRCV[half], bRCVr[2 * half], bRCVr[2 * half + 1]])
                jq = proj(win_d[h], ht_rhs, KC, [bHT, bHTs])
                S.op("act", [lambda jq=jq, h=h: nc.scalar.activation(out=Q[:, 0:NP], in_=PB[jq][:], func=AF.Silu, bias=bin_col(h), scale=1.0),
                             lambda jq=jq, h=h: nc.scalar.activation(out=Q[:, NP:NT], in_=PSMv[jq], func=AF.Silu, bias=bin_col(h), scale=1.0)],
                     reads=[bPB[jq], bPSM[jq], bVEC], writes=[bQ])
                mod_job(); mod_job()
                jf = proj(win_d[8 + h], ht_rhs, KC, [bHT, bHTs])
                mod_job(); mod_job()
                S.op("act", [lambda jf=jf, h=h: nc.scalar.activation(out=F[:, 0:NP], in_=PB[jf][:], func=AF.Sigmoid, bias=bin_col(8 + h), scale=1.0),
                             lambda jf=jf, h=h: nc.scalar.activation(out=F[:, NP:NT], in_=PSMv[jf], func=AF.Sigmoid, bias=bin_col(8 + h), scale=1.0)],
                     reads=[bPB[jf], bPSM[jf], bVEC], writes=[bF])
                S.op("dve", lambda h=h: nc.vector.tensor_scalar(out=F, in0=F, scalar1=OML[:, h:h + 1], scalar2=LBv[:, h:h + 1], op0=ALU.mult, op1=ALU.add),
                     reads=[bSMT], writes=[bF])
                ji = proj(win_d[16 + h], ht_rhs, KC, [bHT, bHTs])
                mod_job()
                S.op("act", [lambda ji=ji, h=h: nc.scalar.activation(out=V[:, 0:NP], in_=PB[ji][:], func=AF.Identity, bias=bin_col(16 + h), scale=1.0),
                             lambda ji=ji, h=h: nc.scalar.activation(out=V[:, NP:NT], in_=PSMv[ji], func=AF.Identity, bias=bin_col(16 + h), scale=1.0)],
                     reads=[bPB[ji], bPSM[ji], bVEC], writes=[bV])
                jg = proj(win_d[24 + h], ht_rhs, KC, [bHT, bHTs])
                mod_job()
                S.op("act", [lambda jg=jg, h=h: nc.scalar.activation(out=GATE[:, 0:NP], in_=PB[jg][:], func=AF.Silu, bias=bin_col(24 + h), scale=1.0),
                             lambda jg=jg, h=h: nc.scalar.activation(out=GATE[:, NP:NT], in_=PSMv[jg], func=AF.Silu, bias=bin_col(24 + h), scale=1.0)],
                     reads=[bPB[jg], bPSM[jg], bVEC], writes=[bGT])
                alias_into(bTMP[0], bSb)
                LF = TMP[0][:, 0:NP]; Bc = TMP[1][:, 0:NP]; E = RSTD[:, 0:NP]
                S.op("act", lambda: nc.scalar.activation(out=LF, in_=F[:, 0:NP], func=AF.Ln), reads=[bF], writes=[bTMP[0]])
                S.op("dve", lambda: nc.vector.tensor_tensor_scan(out=Bc, data0=CMASK, data1=LF, initial=0.0, op0=ALU.mult, op1=ALU.add),
                     reads=[bTMP[0], bSQ[0]], writes=[bTMP[1]])
                S.op("act", lambda: nc.scalar.activation(out=E, in_=Bc, func=AF.Exp), reads=[bTMP[1]], writes=[bRSTD])
                S.op("dve", lambda: nc.vector.tensor_tensor(out=QTb, in0=Q[:, 0:NP], in1=E, op=ALU.mult), reads=[bRSTD, bQ], writes=[bQTb])
                S.op("act", lambda: nc.scalar.activation(out=EBLt, in_=Bc.rearrange("p (c t) -> p c t", t=32)[:, :, 31], func=AF.Exp), reads=[bTMP[1]], writes=[bSMT])
                S.op("act", lambda: nc.scalar.activation(out=LF, in_=Bc, func=AF.Exp, scale=-1.0), reads=[bTMP[1]], writes=[bTMP[0]])
                S.op("dve", lambda: nc.vector.tensor_copy(out=FS, in_=F[:, NP:NT]), reads=[bF], writes=[bSMT])
                S.op("dve", lambda: nc.vector.tensor_scalar(out=F, in0=F, scalar1=-1.0, scalar2=1.0, op0=ALU.mult, op1=ALU.add), reads=[], writes=[bF])
                S.op("dve", lambda: nc.vector.tensor_tensor(out=LF, in0=F[:, 0:NP], in1=LF, op=ALU.mult), reads=[bF], writes=[bTMP[0]])
                S.op("dve", lambda: nc.vector.tensor_tensor(out=F[:, 0:NP].rearrange("p (c t) -> p c t", t=32), in0=LF.rearrange("p (c t) -> p c t", t=32),
                                                            in1=EBLt.unsqueeze(2).to_broadcast([128, 32, 32]), op=ALU.mult),
                     reads=[bTMP[0], bSMT], writes=[bF])
                S.op("act", lambda: nc.scalar.copy(out=KTb, in_=LF), reads=[bTMP[0]], writes=[bKTb])
                transposes_to(F, bF, KHT2, bKHT2)
                transposes_to(V, bV, VT2, bVT2)
                for b_ in bSb:
                    alias_into(b_, [bTMP[0]])
                S.op("act", lambda: nc.scalar.copy(out=Sb[0], in_=SR[0]), reads=[bSR[0]], writes=[bSb[0]])
                def pe_ds(t):
                    a_ = t % 2
                    S.op("pool", lambda: nc.gpsimd.tensor_tensor(out=VTm, in0=VT2[:, t, :].unsqueeze(1).to_broadcast([128, 4, 128]),
                                                                 in1=BM.unsqueeze(2).to_broadcast([128, 4, 128]), op=ALU.mult),
                         reads=[bVT2, bVEC], writes=[bVTm])
                    fns = [(lambda cc=cc: nc.tensor.matmul(DSb[a_][:, cc * 128:(cc + 1) * 128], lhsT=KHT2[:, t, :], rhs=VTm[:, cc, :],
                                                           start=True, stop=True)) for cc in range(4)]
                    S.op("pe", fns, reads=[bKHT2, bVTm], writes=[bPB[a_]])

                def pe_att(t):
                    a_ = t % 2
                    S.op("pe", lambda: nc.tensor.matmul(ATp[a_], lhsT=KTb[:, t * 128:(t + 1) * 128], rhs=QTb[:, t * 128:(t + 1) * 128], start=True, stop=True),
                         reads=[bKTb, bQTb], writes=[bPSM[a_]])

                def dve_mask(t):
                    a_ = t % 2
                    S.op("dve", lambda: nc.vector.tensor_tensor(out=ATT[a_], in0=ATp[a_], in1=MASKF, op=ALU.mult), reads=[bPSM[a_], bVEC], writes=[bATT[a_]])

                def dve_chain(t):
                    for cc in range(4):
                        ci = 4 * t + cc
                        S.op("dve", lambda cc=cc, ci=ci: nc.vector.scalar_tensor_tensor(out=SR[(ci + 1) % 8], in0=SR[ci % 8], scalar=EBLt[:, ci:ci + 1],
                                                                                       in1=DSb[t % 2][:, cc * 128:(cc + 1) * 128], op0=ALU.mult, op1=ALU.add),
                             reads=[bSR[ci % 8], bPB[t % 2], bSMT], writes=[bSR[(ci + 1) % 8]])
                        S.op("act", lambda ci=ci: nc.scalar.copy(out=Sb[(ci + 1) % 8], in_=SR[(ci + 1) % 8]), reads=[bSR[(ci + 1) % 8]], writes=[bSb[(ci + 1) % 8]])

                def pe_po(t):
                    a_ = t % 2
                    fns = [lambda: nc.tensor.matmul(POa[a_], lhsT=VT2[:, t, :], rhs=ATT[a_], start=True, stop=False)]
                    for cc in range(4):
                        col = t * 128 + cc * 32
                        fns.append(lambda cc=cc, col=col: nc.tensor.matmul(POa[a_][:, cc * 32:(cc + 1) * 32], lhsT=Sb[(4 * t + cc) % 8], rhs=QTb[:, col:col + 32],
                                                                           start=False, stop=(cc == 3)))
                    S.op("pe", fns, reads=[bVT2, bATT[a_], bQTb] + [bSb[(4 * t + cc) % 8] for cc in range(4)], writes=[bPO[a_]])
                    S.op("act", lambda: nc.scalar.copy(out=O[:, t * 128:(t + 1) * 128], in_=POa[a_]), reads=[bPO[a_]], writes=[bTMP[1]])

                pe_ds(0); pe_att(0); dve_mask(0); dve_chain(0)
                for t in range(8):
                    if t + 1 < 8:
                        pe_ds(t + 1); pe_att(t + 1)
                    pe_po(t)
                    if t + 1 < 8:
                        dve_mask(t + 1); dve_chain(t + 1)
                S.dma(SP, nhp_slot, lambda h=h: nc.sync.dma_start(out=nhp_o[:, h, :], in_=SR[0]), reads=[bSR[0]])
                S.op("pe", [lambda: nc.tensor.transpose(SM[0][0:16, 0:128], V[:, NP:NT], IDF),
                            lambda: nc.tensor.transpose(SM[0][0:16, 128:256], F[:, NP:NT], IDF)], reads=[bV, bF, bVEC], writes=[bPSM[0]])
                S.op("act", lambda: nc.scalar.copy(out=RSTD[0:16, 0:256], in_=SM[0][0:16, 0:256]), reads=[bPSM[0]], writes=[bRSTD])
                for g4 in range(4):
                    a_ = g4 % 2; half = g4 // 2
                    ns_ = [g4 * 4 + i for i in range(4)]
                    sls = [RCV[half][:, (n % 8) * 128:(n % 8 + 1) * 128] for n in ns_]
                    S.op("dve", [(lambda i=i, n=n: nc.vector.tensor_scalar(out=KM4[i], in0=KTOK, scalar1=ID16[:, n:n + 1], scalar2=None, op0=ALU.mult)) for i, n in enumerate(ns_)],
                         reads=[bRSTD, bVEC], writes=[bKM4])
                    S.op("pe", [(lambda i=i, a_=a_: nc.tensor.matmul(OUT4[a_][:, i * 128:(i + 1) * 128], lhsT=KM4[i], rhs=VTOK, start=True, stop=True)) for i in range(4)],
                         reads=[bKM4, bRSTD], writes=[bPB[a_]])
                    S.op("dve", [(lambda i=i, n=n, sl=sl, a_=a_: nc.vector.scalar_tensor_tensor(out=sl, in0=sl, scalar=FS[:, n:n + 1], in1=OUT4[a_][:, i * 128:(i + 1) * 128],
                                                                                            op0=ALU.mult, op1=ALU.add)) for i, (n, sl) in enumerate(zip(ns_, sls))],
                         reads=[bPB[a_], bSMT, bRCV[half]], writes=[bRCVr[g4]])
                    S.op("pe", [(lambda n=n, sl=sl: nc.tensor.matmul(SM[0][:, 256 + n:257 + n], lhsT=sl, rhs=Q[:, NP + n:NP + n + 1], start=True, stop=True)) for n, sl in zip(ns_, sls)],
                         reads=[bRCVr[g4], bQ], writes=[bPSM[0]])
                    if g4 % 2 == 1:
                        S.dma(SP, nhs_slot[half], lambda h=h, half=half: nc.sync.dma_start(out=nhs_o[:, half * 8:half * 8 + 8, h, :], in_=RCV[half].rearrange("p (n d) -> p n d", n=8)),
                              reads=[bRCV[half], bRCVr[2 * half], bRCVr[2 * half + 1]])
                S.op("act", lambda: nc.scalar.copy(out=O[:, NP:NT], in_=SM[0][:, 256:272]), reads=[bPSM[0]], writes=[bTMP[1]])
                S.op("act", lambda: nc.scalar.activation(out=SQ[1][:], in_=O[:], func=AF.Square), reads=[bTMP[1]], writes=[bSQ[1]])
                fns = [(lambda tb=tb: nc.tensor.matmul(PST[:, tb * 512:(tb + 1) * 512], lhsT=ONESB[:], rhs=SQ[1][:, tb * 512:(tb + 1) * 512], start=True, stop=True)) for tb in range(2)]
                fns.append(lambda: nc.tensor.matmul(SM[0][:, 0:NS], lhsT=ONESB[:], rhs=SQ[1][:, NP:NT], start=True, stop=True))
                S.op("pe", fns, reads=[bSQ[1], bCONST], writes=[bPSTa, bPSTb, bPSM[0]])
                S.op("act", [lambda: nc.scalar.activation(out=RSTD[:, 0:NP], in_=PST[:], func=AF.Sqrt, bias=EPSC[:], scale=1.0 / 128),
                             lambda: nc.scalar.activation(out=RSTD[:, NP:NT], in_=SM[0][:, 0:NS], func=AF.Sqrt, bias=EPSC[:], scale=1.0 / 128)],
                     reads=[bPSTa, bPSTb, bPSM[0], bCONST], writes=[bRSTD])
                S.op("dve", lambda: nc.vector.reciprocal(out=RSTD[:], in_=RSTD[:]), reads=[], writes=[bRSTD])
                S.op("dve", lambda: nc.vector.tensor_tensor(out=O[:], in0=O[:], in1=RSTD[:], op=ALU.mult), reads=[bRSTD], writes=[bTMP[1]])
                S.op("dve", lambda h=h: nc.vector.scalar_tensor_tensor(out=OT[:, h, :], in0=O[:], scalar=VEC[:, V_GA + h:V_GA + h + 1], in1=GATE, op0=ALU.mult, op1=ALU.mult),
                     reads=[bTMP[1], bGT, bVEC], writes=[bOT])
            ot_rhs = lambda k, t0, tn: OT[:, k, t0:t0 + tn]
            for n in range(KC):
                j = proj(wout_d[n], ot_rhs, 8, [bOT])
                resid_evac(j, n, 5)
            if stage == 3:
                finish(XT[:], bXT); return

            norm_mod(2)
            for b_ in bG:
                alias_into(b_, arbufs)
            ffn(2, wup2_d, wdn2_d)
            rms_stats()
            for k in range(KC):
                q = k % 2
                S.op("dve", lambda k=k, q=q: nc.vector.tensor_tensor(out=TMP[q][:], in0=XT[:, k, :], in1=RSTD[:], op=ALU.mult), reads=[bXT[k], bRSTD], writes=[bTMP[q]])
                S.op("act", lambda k=k, q=q: nc.scalar.activation(out=XT[:, k, :], in_=TMP[q][:], func=AF.Identity, scale=VEC[:, V_FG + k:V_FG + k + 1]),
                     reads=[bTMP[q], bVEC], writes=[bXT[k]])
            y_slot = S.slot()
            S.dma(SP, y_slot, lambda: nc.sync.dma_start(out=yT_o.rearrange("(k p) t -> p k t", p=128), in_=XT[:]), reads=bXT)
            finish()
            return


        program(DummySched(), True)
        program(S_real, False)
    return nc


def _tiles(W, nk):
    K, N = W.shape
    return np.ascontiguousarray(W.reshape(nk, 128, N // 128, 128).transpose(2, 1, 0, 3)).reshape(N // 128, 128, nk * 128)


def _fm(v):
    return np.ascontiguousarray(np.asarray(v, np.float32).reshape(-1, 128).T)


def prep_inputs(inp):
    f = lambda a: np.asarray(a, np.float32)
    x_prompt = f(inp["x_prompt"]); x_sample = f(inp["x_sample"])[:, 0, :]
    c_prompt = f(inp["c_prompt"]); c_sample = f(inp["c_sample"])
    wada = _tiles(f(inp["w_ada"])[0], 16)

    def up_tiles(w):
        t = _tiles(f(w)[0], 16)
        o = np.empty_like(t)
        o[0::2] = t[:HC]; o[1::2] = t[HC:]
        return o

    def dn_tiles(w):
        w = f(w)[0]
        return np.concatenate([_tiles(w[g * GJ * 128:(g + 1) * GJ * 128], GJ) for g in range(NG)], 0)
    wup1 = up_tiles(inp["w_f1_up"]); wdn1 = dn_tiles(inp["w_f1_down"])
    wup2 = up_tiles(inp["w_f2_up"]); wdn2 = dn_tiles(inp["w_f2_down"])
    win = _tiles(f(inp["w_in"])[0], 16)
    wo = f(inp["w_out"])[0]
    wout = np.concatenate([_tiles(wo[:1024], 8), _tiles(wo[1024:], 8)], 0)
    vec0 = np.zeros((128, NV), np.float32)
    vec0[:, V_BADA:V_BADA + 144] = _fm(f(inp["b_ada"])[0])
    vec0[:, V_BIN:V_BIN + 48] = _fm(f(inp["b_in"])[0])
    vec0[:, V_NG:V_NG + 48] = _fm(f(inp["norm_g"])[0].reshape(-1))
    vec0[:, V_FG:V_FG + 16] = _fm(f(inp["final_g"]))
    vec0[:, V_GA:V_GA + 8] = _fm(f(inp["g_norm_a"])[0])
    vec0[:, V_CB:V_CB + 8] = _fm(f(inp["conv_b"])[0])
    vec0[:, V_LNG:V_LNG + 8] = _fm(f(inp["ln_g"])[0])
    vec0[:, V_LNB:V_LNB + 8] = _fm(f(inp["ln_b"])[0])
    lb = f(inp["lb_logits"])
    vec0[:, V_LB:V_LB + 8] = _fm(lb[0]); vec0[:, V_LB + 8:V_LB + 16] = _fm(lb[1])
    cw = f(inp["conv_w"])[0]
    vec0[:, V_CW:V_CW + 248] = cw.reshape(CW, 8, 128).transpose(2, 1, 0).reshape(128, 248)
    vec0[:, V_ID:V_ID + 128] = np.eye(128, dtype=np.float32)
    s_i = np.arange(128)[:, None]; t_i = np.arange(128)[None, :]
    vec0[:, V_MASK:V_MASK + 128] = ((s_i // 32 == t_i // 32) & (s_i <= t_i)).astype(np.float32)
    sh = f(inp["state_hgrn"])[0]
    sc = f(inp["state_conv"])[0]
    maps = []
    for c in range(N_CORES):
        seq, half = c // 2, c % 2
        xo = x_prompt[seq, half * NP:(half + 1) * NP]
        xs = x_sample[c * NS:(c + 1) * NS]
        xT = np.ascontiguousarray(np.concatenate([xo, xs], 0).T)
        xTp = np.zeros((D, NT), np.float32)
        if half == 1:
            xTp[:, :NP] = x_prompt[seq, 0:NP].T
        cT = np.ascontiguousarray(np.concatenate([c_sample[c * NS:(c + 1) * NS], c_prompt[seq:seq + 1]], 0).T)
        vec = vec0.copy()
        vec[:, V_M] = float(half)
        if half == 0:
            vec[:, V_OH + seq] = 1.0
        else:
            vec[:, V_SEL + seq] = 1.0
        shg = np.ascontiguousarray(sh[c * NS:(c + 1) * NS].transpose(2, 0, 1, 3))
        scv = np.ascontiguousarray(sc[c * NS:(c + 1) * NS].reshape(NS, HIST, 8, 128).transpose(3, 2, 0, 1))
        maps.append({"xT": xT, "xTp": xTp, "cT": cT, "vec": vec, "wada": wada, "wup1": wup1, "wdn1": wdn1, "wup2": wup2,
                     "wdn2": wdn2, "win": win, "wout": wout, "shg": shg, "scv": scv})
    return maps


def kernel(**inputs):
    maps = prep_inputs(inputs)
    nc = build_nc()
    res = run_bass_kernel_spmd(nc, maps, core_ids=list(range(N_CORES)))
    r = res.results
    y_prompt = np.empty((4, 2048, D), np.float32); y_sample = np.empty((128, 1, D), np.float32)
    nhp = np.empty((1, 4, NH, 128, 128), np.float32); nhs = np.empty((1, 128, NH, 128, 128), np.float32)
    ncp = np.empty((1, 4, HIST, 1024), np.float32); ncs = np.empty((1, 128, HIST, 1024), np.float32)
    for c in range(N_CORES):
        seq, half = c // 2, c % 2
        yT = r[c]["yT"]
        y_prompt[seq, half * NP:(half + 1) * NP] = yT[:, :NP].T
        y_sample[c * NS:(c + 1) * NS, 0] = yT[:, NP:].T
        nhs[0, c * NS:(c + 1) * NS] = r[c]["nhs"].transpose(1, 2, 0, 3)
        ncs[0, c * NS:(c + 1) * NS] = r[c]["ncs"].transpose(2, 3, 1, 0).reshape(NS, HIST, 1024)
        if half == 1:
            nhp[0, seq] = r[c]["nhp"].transpose(1, 0, 2)
            ncp[0, seq] = r[c]["ncp"].transpose(2, 1, 0).reshape(HIST, 1024)
    return (y_prompt, y_sample, nhp, nhs, ncp, ncs)
```
